# Optimizing a Trainium2 kernel written in Bass

```python
import math
import jax, jax.numpy as jnp
from jax import lax
import numpy as np

D_MODEL = 1024
BATCH = 16
SEQ = 2048
DEPTH = 2

CHUNK = 64
N_MIXERS = 2
N_A = (DEPTH + 1) // 2
N_B = DEPTH // 2
CONV_W = 3
CONV_GROUPS = 16
HG_EXPAND = 128
HG_HEADS = D_MODEL // HG_EXPAND
HG_DV = D_MODEL // HG_HEADS
MEM_LEN = 256
X_HEADS = 4
X_HEAD_DIM = D_MODEL // X_HEADS
D_FF = ((8 * D_MODEL // 3 + 255) // 256) * 256
EPS = 1e-6

kernel_name = "hybrid_shortconv_hgrn2_memxattn"


def rmsnorm(x, g):
    xf = x.astype(jnp.float32)
    y = xf * lax.rsqrt(jnp.mean(xf * xf, axis=-1, keepdims=True) + EPS)
    return (y * g.astype(jnp.float32)).astype(x.dtype)


def short_conv_mixer(h, w_in, w_conv, w_out):
    s = h.shape[1]
    gb, gc, u = jnp.split(h @ w_in, 3, axis=-1)
    u = gc * u
    up = jnp.pad(u, ((0, 0), (CONV_W - 1, 0), (0, 0)))
    z = sum(w_conv[j] * up[:, j:j + s] for j in range(CONV_W))
    return (gb * z) @ w_out


def _gla_chunk_step(state, inp):
    q, k, v, logg = inp
    c = q.shape[2]
    b = jnp.cumsum(logg, axis=2)
    o_inter = jnp.einsum('bhtk,bhkv->bhtv', q * jnp.exp(b), state)
    tri = jnp.tril(jnp.ones((c, c), dtype=bool))
    dec = jnp.where(tri[None, None, :, :, None],
                    b[:, :, :, None, :] - b[:, :, None, :, :], -jnp.inf)
    attn = jnp.einsum('bhtk,bhsk,bhtsk->bhts', q, k, jnp.exp(dec))
    o_intra = jnp.einsum('bhts,bhsv->bhtv', attn, v)
    b_last = b[:, :, -1:, :]
    new_state = (jnp.exp(b_last[:, :, 0, :])[..., None] * state
                 + jnp.einsum('bhsk,bhsv->bhkv', k * jnp.exp(b_last - b), v))
    return new_state, o_inter + o_intra


def hgrn2_mixer(h, w_in, w_out, g_norm, lb):
    bsz, s, _ = h.shape
    q, f, i, og = jnp.split(h @ w_in, 4, axis=-1)
    q = jax.nn.silu(q.astype(jnp.float32))
    lbf = lb.astype(jnp.float32)
    g = lbf + (1.0 - lbf) * jax.nn.sigmoid(f.astype(jnp.float32))
    k = 1.0 - g
    logg = jnp.log(g)
    nc = s // CHUNK

    def to_chunks(t, d):
        return t.astype(jnp.float32).reshape(bsz, nc, CHUNK, HG_HEADS, d).transpose(1, 0, 3, 2, 4)

    xs = (to_chunks(q, HG_EXPAND), to_chunks(k, HG_EXPAND), to_chunks(i, HG_DV), to_chunks(logg, HG_EXPAND))
    s0 = jnp.zeros((bsz, HG_HEADS, HG_EXPAND, HG_DV), jnp.float32)
    _, o = lax.scan(_gla_chunk_step, s0, xs)
    o = o.transpose(1, 0, 3, 2, 4).reshape(bsz, s, HG_HEADS, HG_DV)
    o = rmsnorm(o, g_norm.reshape(HG_HEADS, HG_DV))
    o = o.reshape(bsz, s, D_MODEL).astype(h.dtype) * jax.nn.silu(og)
    return o @ w_out


def memory_cross_attn(h, mem_n, w_q, w_kv, w_o):
    bsz, s, _ = h.shape
    q = (h @ w_q).reshape(bsz, s, X_HEADS, X_HEAD_DIM)
    k, v = jnp.split(mem_n @ w_kv, 2, axis=-1)
    k = k.reshape(bsz, -1, X_HEADS, X_HEAD_DIM)
    v = v.reshape(bsz, -1, X_HEADS, X_HEAD_DIM)
    sc = jnp.einsum('bshd,bmhd->bhsm', q.astype(jnp.float32), k.astype(jnp.float32)) * (X_HEAD_DIM ** -0.5)
    p = jax.nn.softmax(sc, axis=-1).astype(h.dtype)
    o = jnp.einsum('bhsm,bmhd->bshd', p, v).reshape(bsz, s, D_MODEL)
    return o @ w_o


def swiglu(h, w_in, w_out):
    a, b = jnp.split(h @ w_in, 2, axis=-1)
    return (jax.nn.silu(a) * b) @ w_out


def setup_inputs(seed: int = 0) -> dict:
    key = jax.random.key(seed)
    ks = jax.random.split(key, 24)
    d = D_MODEL

    def w(k, shape, fan_in):
        return jax.random.normal(k, shape, jnp.float32) * fan_in ** -0.5

    def gain(k, shape):
        return 1.0 + 0.02 * jax.random.normal(k, shape, jnp.float32)

    return {
        "x": jax.random.normal(ks[0], (BATCH, SEQ, d), jnp.float32),
        "mem": jax.random.normal(ks[1], (BATCH, MEM_LEN, d), jnp.float32),
        "norm_mix": gain(ks[2], (DEPTH, d)),
        "conv_w_in": w(ks[3], (N_A, d, 3 * d), d),
        "conv_w": w(ks[4], (N_A, CONV_W, d), CONV_W),
        "conv_w_out": w(ks[5], (N_A, d, d), d),
        "hgrn_w_in": w(ks[6], (N_B, d, 4 * d), d),
        "hgrn_w_out": w(ks[7], (N_B, d, d), d),
        "hgrn_norm": gain(ks[8], (N_B, d)),
        "hgrn_lb": 0.5 * jax.random.normal(ks[9], (DEPTH, d), jnp.float32),
        "norm_xattn": gain(ks[10], (DEPTH, d)),
        "norm_mem": gain(ks[11], (DEPTH, d)),
        "xattn_w_q": w(ks[12], (DEPTH, d, d), d),
        "xattn_w_kv": w(ks[13], (DEPTH, d, 2 * d), d),
        "xattn_w_o": w(ks[14], (DEPTH, d, d), d),
        "norm_ffn": gain(ks[15], (DEPTH, d)),
        "ffn_w_in": w(ks[16], (DEPTH, d, 2 * D_FF), d),
        "ffn_w_out": w(ks[17], (DEPTH, D_FF, d), D_FF),
        "final_norm": gain(ks[18], (d,)),
    }


def reference(x, mem, norm_mix, conv_w_in, conv_w, conv_w_out, hgrn_w_in, hgrn_w_out, hgrn_norm, hgrn_lb,
              norm_xattn, norm_mem, xattn_w_q, xattn_w_kv, xattn_w_o, norm_ffn, ffn_w_in, ffn_w_out, final_norm):
    lb_all = jnp.cumsum(jax.nn.softmax(hgrn_lb.astype(jnp.float32), axis=0), axis=0)
    lb_all = lb_all - lb_all[0:1]
    for i in range(DEPTH):
        h = rmsnorm(x, norm_mix[i])
        if i % N_MIXERS == 0:
            j = i // N_MIXERS
            y = short_conv_mixer(h, conv_w_in[j], conv_w[j], conv_w_out[j])
        else:
            j = i // N_MIXERS
            y = hgrn2_mixer(h, hgrn_w_in[j], hgrn_w_out[j], hgrn_norm[j], lb_all[i])
        x = x + y
        h = rmsnorm(x, norm_xattn[i])
        x = x + memory_cross_attn(h, rmsnorm(mem, norm_mem[i]), xattn_w_q[i], xattn_w_kv[i], xattn_w_o[i])
        h = rmsnorm(x, norm_ffn[i])
        x = x + swiglu(h, ffn_w_in[i], ffn_w_out[i])
    return rmsnorm(x, final_norm)
```

```python
import numpy as np
from contextlib import ExitStack
import concourse.bass as bass
import concourse.mybir as mybir
from concourse.bass_utils import run_bass_kernel_spmd

F32 = mybir.dt.float32
BF16 = mybir.dt.bfloat16
ALU = mybir.AluOpType
AF = mybir.ActivationFunctionType

D = 1024
DFF = 2816
MEM = 256
EPS = 1e-6
NSLOT = 4
EPOCH = 4096
DEBUG = False
ENGINES = ("pe", "act", "dve", "pool", "sp")


class Op:
    __slots__ = ("eng", "fn", "reads", "writes", "dma", "idx", "waits", "signal", "count", "semkey")

    def __init__(self, eng, fn, reads, writes, dma):
        self.eng = eng
        self.fn = fn
        self.reads = tuple(reads)
        self.writes = tuple(writes)
        self.dma = dma
        self.waits = {}
        self.signal = False
        self.count = 0
        self.semkey = None


class Prog:
    def __init__(self):
        self.ops = []

    def add(self, eng, fn, reads=(), writes=(), dma=None):
        op = Op(eng, fn, reads, writes, dma)
        op.idx = len(self.ops)
        self.ops.append(op)
        return op

    def pe(self, fn, reads=(), writes=()):
        return self.add("pe", fn, reads, writes)

    def act(self, fn, reads=(), writes=()):
        return self.add("act", fn, reads, writes)

    def dve(self, fn, reads=(), writes=()):
        return self.add("dve", fn, reads, writes)

    def pool(self, fn, reads=(), writes=()):
        return self.add("pool", fn, reads, writes)

    def dma(self, q, stream, fn, reads=(), writes=()):
        return self.add(q, fn, reads, writes, dma=stream)

    def analyze(self, force_signal=()):
        last_writer = {}
        readers = {}
        need = [None] * len(self.ops)
        for op in self.ops:
            deps = set()
            for t in op.reads:
                w = last_writer.get(t)
                if w is not None:
                    deps.add(w)
            for t in op.writes:
                w = last_writer.get(t)
                if w is not None:
                    deps.add(w)
                deps.update(readers.get(t, ()))
            deps.discard(op.idx)
            best = {}
            nd = []
            for d in deps:
                a = self.ops[d]
                if a.dma is not None:
                    nd.append(d)
                    continue
                if op.dma is None and a.eng == op.eng and a.eng == "pe":
                    continue
                if d > best.get(a.eng, -1):
                    best[a.eng] = d
            nd.extend(best.values())
            for d in nd:
                self.ops[d].signal = True
            need[op.idx] = nd
            for t in op.writes:
                last_writer[t] = op.idx
                readers[t] = []
            for t in op.reads:
                if t not in op.writes:
                    readers.setdefault(t, []).append(op.idx)
        for op in force_signal:
            op.signal = True
        cnt = {}
        raw = {}
        for op in self.ops:
            if not op.signal:
                continue
            if op.dma is not None:
                key = ("dma", op.dma)
                cnt[key] = cnt.get(key, 0) + 16
                op.count = cnt[key]
            else:
                r = raw.get(op.eng, 0)
                key = ("eng", op.eng, r // EPOCH)
                raw[op.eng] = r + 1
                op.count = (r % EPOCH) + 1
                cnt[key] = op.count
            op.semkey = key
        self.semkeys = list(cnt.keys())
        self.final_counts = cnt
        waited = {e: {} for e in ENGINES}
        nw = 0
        for op in self.ops:
            w = {}
            for d in need[op.idx]:
                a = self.ops[d]
                if a.count > w.get(a.semkey, 0):
                    w[a.semkey] = a.count
            ww = {}
            for k, v in w.items():
                if waited[op.eng].get(k, 0) < v:
                    waited[op.eng][k] = v
                    ww[k] = v
                    nw += 1
            op.waits = ww
        self.n_waits = nw

    def emit(self, sems, block, final_waits=()):
        per = {e: [o for o in self.ops if o.eng == e] for e in ENGINES}

        def run(name):
            def body(eng):
                for op in per[name]:
                    for k, v in op.waits.items():
                        eng.wait_ge(sems[k], v)
                    ins = op.fn(eng)
                    if op.signal:
                        ins.then_inc(sems[op.semkey], 16 if op.dma is not None else 1)
                if name == "sp":
                    for k, v in final_waits:
                        eng.wait_ge(sems[k], v)
            return body

        block.tensor(run("pe"))
        block.scalar(run("act"))
        block.vector(run("dve"))
        block.gpsimd(run("pool"))
        block.sync(run("sp"))


FFN_GROUPS = ((0, 8), (8, 8), (16, 6))


def layer_pieces(l):
    out = []
    if l % 2 == 0:
        j = l // 2
        for m in range(8):
            blocks = [("conv_w_in", j, 0, 8, sec * 1024 + m * 128, 128, sec * 128) for sec in range(3)]
            out.append(("conv_in", m, blocks))
        for p in range(2):
            out.append(("out", ("conv", p, 8), [("conv_w_out", j, 0, 8, p * 512, 512, 0)]))
    else:
        j = l // 2
        for p in range(2):
            out.append(("og", p, [("hgrn_w_in", j, 0, 8, 3072 + p * 512, 512, 0)]))
        for hg in range(2):
            out.append(("gla_q", hg, [("hgrn_w_in", j, 0, 8, hg * 512, 512, 0)]))
            out.append(("gla_f", hg, [("hgrn_w_in", j, 0, 8, 1024 + hg * 512, 512, 0)]))
            out.append(("gla_i", hg, [("hgrn_w_in", j, 0, 8, 2048 + hg * 512, 512, 0)]))
        for p in range(2):
            out.append(("out", ("hgrn", p, 8), [("hgrn_w_out", j, 0, 8, p * 512, 512, 0)]))
    for p in range(4):
        out.append(("xa_kv", p, [("xattn_w_kv", l, 0, 8, p * 512, 512, 0)]))
    for p in range(2):
        out.append(("xa_q", p, [("xattn_w_q", l, 0, 8, p * 512, 512, 0)]))
    for p in range(2):
        out.append(("out", ("xa", p, 8), [("xattn_w_o", l, 0, 8, p * 512, 512, 0)]))
    for (f0, nf) in FFN_GROUPS:
        for q in range(nf // 2):
            fa, fb = f0 + 2 * q, f0 + 2 * q + 1
            blocks = [("ffn_w_in", l, 0, 8, fa * 128, 128, 0),
                      ("ffn_w_in", l, 0, 8, fb * 128, 128, 128),
                      ("ffn_w_in", l, 0, 8, DFF + fa * 128, 128, 256),
                      ("ffn_w_in", l, 0, 8, DFF + fb * 128, 128, 384)]
            out.append(("ffn_in", 2 * q, blocks))
        for p in range(2):
            out.append(("out", ("ffn", p, nf), [("ffn_w_out", l, f0 * 128, nf, p * 512, 512, 0)]))
    return out


def all_pieces():
    res = []
    for l in range(2):
        res.append(layer_pieces(l))
    return res


def pack_weights(inputs):
    pcs = all_pieces()
    n = sum(len(p) for p in pcs)
    wp = np.zeros((n, 128, 8, 512), np.float32)
    j = 0
    for lp in pcs:
        for (_, _, blocks) in lp:
            for (wn, li, r0, nkc, c0, ncol, dc) in blocks:
                w = inputs[wn][li]
                blk = w[r0:r0 + nkc * 128, c0:c0 + ncol].reshape(nkc, 128, ncol).transpose(1, 0, 2)
                wp[j, :, :nkc, dc:dc + ncol] = blk
            j += 1
    return wp.reshape(n, 128, 8 * 512)


def _fm(v):
    return np.asarray(v, np.float32).reshape(8, 128).T


CST_COLS = {}


def pack_consts(inputs):
    cols = []
    off = [0]

    def put(name, arr):
        CST_COLS[name] = (off[0], arr.shape[1])
        off[0] += arr.shape[1]
        cols.append(arr)

    for l in range(2):
        put("g_mix%d" % l, _fm(inputs["norm_mix"][l]))
        put("g_xa%d" % l, _fm(inputs["norm_xattn"][l]))
        put("g_mem%d" % l, _fm(inputs["norm_mem"][l]))
        put("g_ffn%d" % l, _fm(inputs["norm_ffn"][l]))
    put("g_fin", _fm(inputs["final_norm"]))
    put("gn", _fm(inputs["hgrn_norm"][0]))
    for jj in range(3):
        put("cw%d" % jj, _fm(inputs["conv_w"][0][jj]))
    return np.ascontiguousarray(np.concatenate(cols, axis=1))


def const_tables():
    s = np.arange(128)
    same = (s[:, None] // 64) == (s[None, :] // 64)
    tri = (same & (s[:, None] <= s[None, :])).astype(np.float32)
    ident = np.eye(128, dtype=np.float32)
    cm = np.zeros((128, 2), np.float32)
    cm[:64, 0] = 1.0
    cm[64:, 1] = 1.0
    return np.ascontiguousarray(np.concatenate([ident, tri, cm], axis=1))


def build_program(S, NSEQ, nsub=6, final=True):
    NT = S // 512
    NL = S // 128
    pcs = all_pieces()
    npieces = sum(len(p) for p in pcs)
    dummy = {"norm_mix": np.zeros((2, D)), "norm_xattn": np.zeros((2, D)), "norm_mem": np.zeros((2, D)),
             "norm_ffn": np.zeros((2, D)), "final_norm": np.zeros(D), "hgrn_norm": np.zeros((1, D)),
             "conv_w": np.zeros((1, 3, D))}
    ncst = pack_consts(dummy).shape[1]

    nc = bass.Bass("TRN2", target_bir_lowering=False)
    xd = nc.dram_tensor("xT", [NSEQ, D, S], F32, kind="ExternalInput").ap()
    md = nc.dram_tensor("memT", [NSEQ, D, MEM], F32, kind="ExternalInput").ap()
    wpd = nc.dram_tensor("wp", [npieces, 128, 8 * 512], F32, kind="ExternalInput").ap()
    cstd = nc.dram_tensor("cst", [128, ncst], F32, kind="ExternalInput").ap()
    ctabd = nc.dram_tensor("ctab", [128, 258], F32, kind="ExternalInput").ap()
    lbd = nc.dram_tensor("lb", [2, D], F32, kind="ExternalInput")
    od = nc.dram_tensor("outT", [NSEQ, D, S], F32, kind="ExternalOutput").ap()

    P = Prog()
    es = ExitStack()
    sb = lambda name, shape, dt: es.enter_context(nc.sbuf_tensor(name, shape, dt))
    xT = sb("xT_sb", [128, 8, S], F32)
    hT = sb("hT_sb", [128, 8, S], BF16)
    aT = sb("aT_sb", [128, 8, S], BF16)
    slots = [sb("slot%d" % i, [128, 8, 512], BF16) for i in range(NSLOT)]
    NF, NB = 10, 12
    Fp = [sb("F%d" % i, [128, 512], F32) for i in range(NF)]
    Bp = [sb("B%d" % i, [128, 512], BF16) for i in range(NB)]
    cst = sb("cst_sb", [128, ncst], F32)
    ctab = sb("ctab_sb", [128, 258], F32)
    identb = sb("identb", [128, 128], BF16)
    onesb = sb("onesb", [128, 128], BF16)
    maskb = sb("maskb", [128, 128], BF16)
    oml = sb("oml", [128, D], F32)
    Sst = sb("Sst", [128, 512], F32)
    vcar = [sb("vcar%d" % i, [128, 514], F32) for i in range(2)]
    ebl = [sb("ebl%d" % i, [128, 8], F32) for i in range(2)]
    psb = [es.enter_context(nc.psum_tensor("ps%d" % i, [128, 512], F32)) for i in range(8)]
    ps_state = {"n": 0}
    DBG_T = {}
    if DEBUG:
        DBG_T["s1"] = sb("dbg_s1", [128, 512], F32)
        DBG_T["e"] = sb("dbg_e", [128, 8], F32)

    def nps():
        b = ps_state["n"] % 8
        ps_state["n"] += 1
        return b

    def cc(name, c=None):
        o, n = CST_COLS[name]
        if c is None:
            return cst[:, o:o + n]
        return cst[:, o + c:o + c + 1]

    tri32 = ctab[:, 128:256]
    cm32 = ctab[:, 256:258]

    P.dma("sp", "cst0", lambda e: e.dma_start(out=cst[:], in_=cstd), writes=["cst"])
    P.dma("sp", "cst1", lambda e: e.dma_start(out=ctab[:], in_=ctabd), writes=["ctab"])
    lb0 = bass.AP(lbd, 0, [[0, 128], [1, D]])
    P.dma("sp", "cst2", lambda e: e.dma_start(out=oml[:], in_=lb0), writes=["oml"])
    for hf in range(2):
        lb1 = bass.AP(lbd, D + hf * 512, [[0, 128], [1, 512]])
        P.dma("sp", "cst3_%d" % hf, lambda e, hf=hf, lb1=lb1: e.dma_start(out=Fp[4 + hf][:], in_=lb1), writes=[("F", 4 + hf)])
    P.dve(lambda e: e.tensor_copy(out=identb[:], in_=ctab[:, 0:128]), reads=["ctab"], writes=["identb"])
    P.dve(lambda e: e.tensor_copy(out=maskb[:], in_=ctab[:, 128:256]), reads=["ctab"], writes=["maskb"])
    P.dve(lambda e: e.memset(onesb[:], 1.0), writes=["onesb"])
    for hf in range(2):
        P.dve(lambda e, hf=hf: e.tensor_tensor(out=oml[:, hf * 512:(hf + 1) * 512], in0=oml[:, hf * 512:(hf + 1) * 512],
                                               in1=Fp[4 + hf][:], op=ALU.subtract),
              reads=["oml", ("F", 4 + hf)], writes=["oml"])
    P.act(lambda e: e.activation(out=oml[:], in_=oml[:], func=AF.Sigmoid), reads=["oml"], writes=["oml"])

    def xtok(c, tt):
        return ("x", c, tt)

    def htok(tt):
        return ("h", tt)

    def atok(c, tt):
        return ("a", c, tt)

    def norm_tile(src_fn, g_name, N, reads, out_fn, out_writes, ftmp, btmps):
        pb = nps()
        for c in range(8):
            bi = btmps[c % len(btmps)]
            P.act(lambda e, c=c, bi=bi: e.activation(out=Bp[bi][:, 0:N], in_=src_fn(c), func=AF.Square),
                  reads=[reads(c)], writes=[("B", bi)])
            P.pe(lambda e, c=c, bi=bi, pb=pb: e.matmul(psb[pb][:, 0:N], lhsT=onesb[:], rhs=Bp[bi][:, 0:N],
                                                     start=(c == 0), stop=(c == 7)),
                 reads=[("B", bi), "onesb"], writes=[("ps", pb)])
        P.act(lambda e, pb=pb: e.activation(out=Fp[ftmp][:, 0:N], in_=psb[pb][:, 0:N], func=AF.Sqrt,
                                            bias=EPS, scale=1.0 / D),
              reads=[("ps", pb)], writes=[("F", ftmp)])
        P.dve(lambda e: e.reciprocal(out=Fp[ftmp][:, 0:N], in_=Fp[ftmp][:, 0:N]),
              reads=[("F", ftmp)], writes=[("F", ftmp)])
        for c in range(8):
            P.dve(lambda e, c=c: e.scalar_tensor_tensor(out=out_fn(c), in0=src_fn(c), scalar=cc(g_name, c),
                                                        in1=Fp[ftmp][:, 0:N], op0=ALU.mult, op1=ALU.mult),
                  reads=[reads(c), ("F", ftmp), "cst"], writes=[out_writes(c)])

    def norm_phase(g_name):
        for tt in range(NT):
            ts = slice(tt * 512, (tt + 1) * 512)
            norm_tile(lambda c, ts=ts: xT[:, c, ts], g_name, 512, lambda c, tt=tt: xtok(c, tt),
                      lambda c, ts=ts: hT[:, c, ts], lambda c, tt=tt: htok(tt),
                      ftmp=tt % 2, btmps=[0, 1, 2, 3])

    def out_stage(slot, stok, info):
        _, p, nkc = info
        for tt in range(NT):
            ts = slice(tt * 512, (tt + 1) * 512)
            for mo in range(4):
                pb = nps()
                ca = p * 4 + mo
                for kc in range(nkc):
                    P.pe(lambda e, kc=kc, mo=mo, pb=pb, ts=ts: e.matmul(
                        psb[pb][:], lhsT=slot[:, kc, mo * 128:(mo + 1) * 128], rhs=aT[:, kc, ts],
                        start=(kc == 0), stop=(kc == nkc - 1)),
                        reads=[stok, atok(kc, tt)], writes=[("ps", pb)])
                P.dve(lambda e, ca=ca, pb=pb, ts=ts: e.tensor_tensor(out=xT[:, ca, ts], in0=psb[pb][:],
                                                                     in1=xT[:, ca, ts], op=ALU.add),
                      reads=[("ps", pb), xtok(ca, tt)], writes=[xtok(ca, tt)])

    def conv_in(slot, stok, m):
        for tt in range(NT):
            ts = slice(tt * 512, (tt + 1) * 512)
            vb, vn = vcar[tt % 2], vcar[(tt + 1) % 2]
            vtk, vntk = ("vcar", tt % 2), ("vcar", (tt + 1) % 2)
            if tt == 0:
                P.dve(lambda e, vb=vb: e.memset(vb[:, 0:2], 0.0), writes=[vtk])
            pbs = []
            for sec in range(3):
                pb = nps()
                pbs.append(pb)
                for kc in range(8):
                    P.pe(lambda e, kc=kc, sec=sec, pb=pb, ts=ts: e.matmul(
                        psb[pb][:], lhsT=slot[:, kc, sec * 128:(sec + 1) * 128], rhs=hT[:, kc, ts],
                        start=(kc == 0), stop=(kc == 7)),
                        reads=[stok, htok(tt)], writes=[("ps", pb)])
            pgb, pgc, pu = pbs
            fu, fz = 2 + (tt % 2), 4 + (tt % 2)
            P.act(lambda e, pu=pu, fu=fu: e.activation(out=Fp[fu][:], in_=psb[pu][:], func=AF.Copy),
                  reads=[("ps", pu)], writes=[("F", fu)])
            P.dve(lambda e, pgc=pgc, fu=fu, vb=vb: e.tensor_tensor(out=vb[:, 2:514], in0=psb[pgc][:], in1=Fp[fu][:],
                                                                   op=ALU.mult),
                  reads=[("ps", pgc), ("F", fu)], writes=[vtk])
            if tt + 1 < NT:
                P.act(lambda e, vb=vb, vn=vn: e.activation(out=vn[:, 0:2], in_=vb[:, 512:514], func=AF.Copy),
                      reads=[vtk], writes=[vntk])
            P.act(lambda e, vb=vb, fz=fz, m=m: e.activation(out=Fp[fz][:], in_=vb[:, 2:514], func=AF.Copy,
                                                            scale=cc("cw2", m)),
                  reads=[vtk, "cst"], writes=[("F", fz)])
            P.dve(lambda e, vb=vb, fz=fz, m=m: e.scalar_tensor_tensor(out=Fp[fz][:], in0=vb[:, 1:513],
                                                                      scalar=cc("cw1", m), in1=Fp[fz][:],
                                                                      op0=ALU.mult, op1=ALU.add),
                  reads=[vtk, ("F", fz), "cst"], writes=[("F", fz)])
            P.dve(lambda e, vb=vb, fz=fz, m=m: e.scalar_tensor_tensor(out=Fp[fz][:], in0=vb[:, 0:512],
                                                                      scalar=cc("cw0", m), in1=Fp[fz][:],
                                                                      op0=ALU.mult, op1=ALU.add),
                  reads=[vtk, ("F", fz), "cst"], writes=[("F", fz)])
            P.dve(lambda e, pgb=pgb, fz=fz, m=m, ts=ts: e.tensor_tensor(out=aT[:, m, ts], in0=psb[pgb][:],
                                                                        in1=Fp[fz][:], op=ALU.mult),
                  reads=[("ps", pgb), ("F", fz)], writes=[atok(m, tt)])

    def ffn_in(slot, stok, lc0):
        for tt in range(NT):
            ts = slice(tt * 512, (tt + 1) * 512)
            for l2 in range(2):
                pa, pbk = nps(), nps()
                for (pb, co) in ((pa, l2 * 128), (pbk, 256 + l2 * 128)):
                    for kc in range(8):
                        P.pe(lambda e, kc=kc, pb=pb, co=co, ts=ts: e.matmul(
                            psb[pb][:], lhsT=slot[:, kc, co:co + 128], rhs=hT[:, kc, ts],
                            start=(kc == 0), stop=(kc == 7)),
                            reads=[stok, htok(tt)], writes=[("ps", pb)])
                fs = 6 + ((2 * tt + l2) % 2)
                lc = lc0 + l2
                P.act(lambda e, pa=pa, fs=fs: e.activation(out=Fp[fs][:], in_=psb[pa][:], func=AF.Silu),
                      reads=[("ps", pa)], writes=[("F", fs)])
                P.dve(lambda e, pbk=pbk, fs=fs, lc=lc, ts=ts: e.tensor_tensor(out=aT[:, lc, ts], in0=psb[pbk][:],
                                                                              in1=Fp[fs][:], op=ALU.mult),
                      reads=[("ps", pbk), ("F", fs)], writes=[atok(lc, tt)])

    def bview(i0, n, inner):
        raise NotImplementedError

    def memn_ap(c):
        return Bp[c // 2][:, (c % 2) * 256:(c % 2) * 256 + 256]

    def memn_tok(c):
        return ("B", c // 2)

    def kT_ap(dch):
        return Bp[4 + dch // 2][:, (dch % 2) * 256:(dch % 2) * 256 + 256]

    def kT_tok(dch):
        return ("B", 4 + dch // 2)

    def v_ap(mc, col0, n):
        half = col0 // 512
        return Bp[8 + mc * 2 + half][:, col0 % 512:col0 % 512 + n]

    def v_tok(mc, col0):
        return ("B", 8 + mc * 2 + col0 // 512)

    def xa_pre(seq, l):
        def mT(c):
            return Fp[c // 2][:, (c % 2) * 256:(c % 2) * 256 + 256]
        for c in range(8):
            P.dma("sp", "mem%d" % c, lambda e, c=c: e.dma_start(out=mT(c), in_=md[seq, c * 128:(c + 1) * 128, :]),
                  writes=[("F", c // 2)])
        norm_tile(mT, "g_mem%d" % l, MEM, lambda c: ("F", c // 2),
                  memn_ap, memn_tok, ftmp=8, btmps=[8, 9, 10, 11])

    def xa_kv(slot, stok, p):
        if p < 2:
            for j in range(4):
                dch = p * 4 + j
                pb = nps()
                for kc in range(8):
                    P.pe(lambda e, kc=kc, j=j, pb=pb: e.matmul(psb[pb][:, 0:MEM], lhsT=slot[:, kc, j * 128:(j + 1) * 128],
                                                             rhs=memn_ap(kc), start=(kc == 0), stop=(kc == 7)),
                         reads=[stok, memn_tok(kc)], writes=[("ps", pb)])
                P.act(lambda e, pb=pb, dch=dch: e.activation(out=kT_ap(dch), in_=psb[pb][:, 0:MEM], func=AF.Copy),
                      reads=[("ps", pb)], writes=[kT_tok(dch)])
        else:
            col0 = (p - 2) * 512
            for mc in range(2):
                pb = nps()
                for kc in range(8):
                    P.pe(lambda e, kc=kc, mc=mc, pb=pb: e.matmul(
                        psb[pb][:], lhsT=memn_ap(kc)[:, mc * 128:(mc + 1) * 128], rhs=slot[:, kc, :],
                        start=(kc == 0), stop=(kc == 7)),
                        reads=[stok, memn_tok(kc)], writes=[("ps", pb)])
                P.act(lambda e, pb=pb, mc=mc, col0=col0: e.activation(out=v_ap(mc, col0, 512), in_=psb[pb][:],
                                                                      func=AF.Copy),
                      reads=[("ps", pb)], writes=[v_tok(mc, col0)])

    def xa_q(slot, stok, p):
        for tt in range(NT):
            ts = slice(tt * 512, (tt + 1) * 512)
            for j in range(4):
                pb = nps()
                for kc in range(8):
                    P.pe(lambda e, kc=kc, j=j, pb=pb, ts=ts: e.matmul(
                        psb[pb][:], lhsT=slot[:, kc, j * 128:(j + 1) * 128], rhs=hT[:, kc, ts],
                        start=(kc == 0), stop=(kc == 7)),
                        reads=[stok, htok(tt)], writes=[("ps", pb)])
                P.act(lambda e, pb=pb, j=j: e.activation(out=Bp[j][:], in_=psb[pb][:], func=AF.Copy),
                      reads=[("ps", pb)], writes=[("B", j)])
            for hh in range(2):
                head = 2 * p + hh
                fpt = 4 + hh
                PT = Fp[fpt][:].bitcast(BF16)
                for mc in range(2):
                    pb = nps()
                    for jj in range(2):
                        dabs = head * 2 + jj
                        P.pe(lambda e, jj=jj, mc=mc, pb=pb, dabs=dabs, hh=hh: e.matmul(
                            psb[pb][:], lhsT=kT_ap(dabs)[:, mc * 128:(mc + 1) * 128], rhs=Bp[2 * hh + jj][:],
                            start=(jj == 0), stop=(jj == 1)),
                            reads=[kT_tok(dabs), ("B", 2 * hh + jj)], writes=[("ps", pb)])
                    P.act(lambda e, pb=pb, mc=mc, PT=PT: e.activation(out=PT[:, mc * 512:(mc + 1) * 512], in_=psb[pb][:],
                                                                      func=AF.Exp, scale=1.0 / 16.0),
                          reads=[("ps", pb)], writes=[("F", fpt)])
                pd = nps()
                for mc in range(2):
                    P.pe(lambda e, mc=mc, pd=pd, PT=PT: e.matmul(psb[pd][:], lhsT=onesb[:],
                                                                 rhs=PT[:, mc * 512:(mc + 1) * 512],
                                                                 start=(mc == 0), stop=(mc == 1)),
                         reads=[("F", fpt), "onesb"], writes=[("ps", pd)])
                frd = 6 + hh
                P.dve(lambda e, pd=pd, frd=frd: e.reciprocal(out=Fp[frd][:], in_=psb[pd][:]),
                      reads=[("ps", pd)], writes=[("F", frd)])
                for jj in range(2):
                    dabs = head * 2 + jj
                    po = nps()
                    for mc in range(2):
                        P.pe(lambda e, mc=mc, po=po, PT=PT, head=head, jj=jj: e.matmul(
                            psb[po][:], lhsT=v_ap(mc, head * 256 + jj * 128, 128),
                            rhs=PT[:, mc * 512:(mc + 1) * 512], start=(mc == 0), stop=(mc == 1)),
                            reads=[("F", fpt), v_tok(mc, head * 256 + jj * 128)], writes=[("ps", po)])
                    P.dve(lambda e, po=po, frd=frd, dabs=dabs, ts=ts: e.tensor_tensor(
                        out=aT[:, dabs, ts], in0=psb[po][:], in1=Fp[frd][:], op=ALU.mult),
                        reads=[("ps", po), ("F", frd)], writes=[atok(dabs, tt)])

    def og_stage(slot, stok, p):
        for tt in range(NT):
            ts = slice(tt * 512, (tt + 1) * 512)
            for j in range(4):
                pb = nps()
                ca = p * 4 + j
                for kc in range(8):
                    P.pe(lambda e, kc=kc, j=j, pb=pb, ts=ts: e.matmul(
                        psb[pb][:], lhsT=slot[:, kc, j * 128:(j + 1) * 128], rhs=hT[:, kc, ts],
                        start=(kc == 0), stop=(kc == 7)),
                        reads=[stok, htok(tt)], writes=[("ps", pb)])
                P.act(lambda e, pb=pb, ca=ca, ts=ts: e.activation(out=aT[:, ca, ts], in_=psb[pb][:], func=AF.Silu),
                      reads=[("ps", pb)], writes=[atok(ca, tt)])

    def gla(sq_, sf_, si_, tq, tf, ti, hg):
        hc = slice(hg * 512, (hg + 1) * 512)
        P.dve(lambda e: e.memset(Sst[:], 0.0), writes=["S"])
        P.dve(lambda e: e.memset(Bp[11][:], 0.0), writes=[("B", 11)])

        def front(tl):
            par = tl % 2
            tk = slice(tl * 128, (tl + 1) * 128)
            tt = tl // 4
            pq, pf, pi_ = nps(), nps(), nps()
            for (pb, sl, stk) in ((pq, sq_, tq), (pf, sf_, tf), (pi_, si_, ti)):
                for kc in range(8):
                    P.pe(lambda e, kc=kc, pb=pb, sl=sl: e.matmul(psb[pb][:], lhsT=hT[:, kc, tk], rhs=sl[:, kc, :],
                                                               start=(kc == 0), stop=(kc == 7)),
                         reads=[stk, htok(tt)], writes=[("ps", pb)])
            bk, bq, bv, bkT, bqT = 0 + par, 2, 3 + par, 5 + par, 7 + par
            P.act(lambda e: e.activation(out=Fp[0][:], in_=psb[pq][:], func=AF.Silu),
                  reads=[("ps", pq)], writes=[("F", 0)])
            P.act(lambda e: e.activation(out=Fp[1][:], in_=psb[pf][:], func=AF.Sigmoid, scale=-1.0),
                  reads=[("ps", pf)], writes=[("F", 1)])
            P.act(lambda e: e.activation(out=Bp[bv][:], in_=psb[pi_][:], func=AF.Copy),
                  reads=[("ps", pi_)], writes=[("B", bv)])
            P.dve(lambda e: e.tensor_tensor(out=Fp[1][:], in0=Fp[1][:], in1=oml[:, hc], op=ALU.mult),
                  reads=[("F", 1), "oml"], writes=[("F", 1)])
            P.act(lambda e: e.activation(out=Fp[2][:], in_=Fp[1][:], func=AF.Ln, scale=-1.0, bias=1.0),
                  reads=[("F", 1)], writes=[("F", 2)])
            pbb = nps()
            P.pe(lambda e: e.matmul(psb[pbb][:], lhsT=tri32, rhs=Fp[2][:], start=True, stop=True),
                 reads=[("F", 2), "ctab"], writes=[("ps", pbb)])
            pbl = nps()
            for hd in range(4):
                P.pe(lambda e, hd=hd: e.matmul(psb[pbl][:, 2 * hd:2 * hd + 2], lhsT=Fp[2][:, hd * 128:(hd + 1) * 128],
                                               rhs=cm32, start=(hd == 0), stop=(hd == 3)),
                     reads=[("F", 2), "ctab"], writes=[("ps", pbl)])
            P.act(lambda e: e.activation(out=ebl[par][:], in_=psb[pbl][:, 0:8], func=AF.Exp),
                  reads=[("ps", pbl)], writes=[("ebl", par)])
            P.act(lambda e: e.activation(out=Fp[3][:], in_=psb[pbb][:], func=AF.Exp, scale=-1.0),
                  reads=[("ps", pbb)], writes=[("F", 3)])
            P.dve(lambda e: e.tensor_tensor(out=Bp[bk][:], in0=Fp[1][:], in1=Fp[3][:], op=ALU.mult),
                  reads=[("F", 1), ("F", 3)], writes=[("B", bk)])
            P.act(lambda e: e.activation(out=Fp[3][:], in_=psb[pbb][:], func=AF.Exp),
                  reads=[("ps", pbb)], writes=[("F", 3)])
            P.dve(lambda e: e.tensor_tensor(out=Bp[bq][:], in0=Fp[0][:], in1=Fp[3][:], op=ALU.mult),
                  reads=[("F", 0), ("F", 3)], writes=[("B", bq)])
            for (src, dst) in ((bk, bkT), (bq, bqT)):
                pT = nps()
                pv = psb[pT][:].bitcast(BF16)
                for hd in range(4):
                    P.pe(lambda e, hd=hd, src=src, pv=pv: e.transpose(pv[:, hd * 128:(hd + 1) * 128],
                                                                      Bp[src][:, hd * 128:(hd + 1) * 128], identb[:]),
                         reads=[("B", src), "identb"], writes=[("ps", pT)])
                P.act(lambda e, pv=pv, dst=dst: e.activation(out=Bp[dst][:], in_=pv[:, 0:512], func=AF.Copy),
                      reads=[("ps", pT)], writes=[("B", dst)])

        def back(tl):
            par = tl % 2
            tk = slice(tl * 128, (tl + 1) * 128)
            tt = tl // 4
            bk, bv, bkT, bqT = 0 + par, 3 + par, 5 + par, 7 + par
            pat = nps()
            for hd in range(4):
                h_ = slice(hd * 128, (hd + 1) * 128)
                P.pe(lambda e, hd=hd, h_=h_: e.matmul(psb[pat][:, h_], lhsT=Bp[bkT][:, h_], rhs=Bp[bqT][:, h_],
                                                      start=(hd == 0), stop=(hd == 3)),
                     reads=[("B", bkT), ("B", bqT)], writes=[("ps", pat)])
            mask4 = maskb[:, :].unsqueeze(1).to_broadcast([128, 4, 128])
            P.dve(lambda e: e.tensor_tensor(out=Bp[9][:].rearrange("p (h t) -> p h t", h=4),
                                            in0=psb[pat][:].rearrange("p (h t) -> p h t", h=4),
                                            in1=mask4, op=ALU.mult),
                  reads=[("ps", pat), "maskb"], writes=[("B", 9)])
            po = nps()
            for hd in range(4):
                h_ = slice(hd * 128, (hd + 1) * 128)
                P.pe(lambda e, hd=hd, h_=h_: e.matmul(psb[po][:, h_], lhsT=Bp[bv][:, h_], rhs=Bp[9][:, h_],
                                                      start=(hd == 0), stop=False),
                     reads=[("B", bv), ("B", 9)], writes=[("ps", po)])
            for c in range(2):
                cs = slice(c * 64, (c + 1) * 64)
                for hd in range(4):
                    h_ = slice(hd * 128, (hd + 1) * 128)
                    oc = slice(hd * 128 + c * 64, hd * 128 + (c + 1) * 64)
                    P.pe(lambda e, hd=hd, h_=h_, oc=oc, c=c: e.matmul(psb[po][:, oc], lhsT=Bp[11][:, h_], rhs=Bp[bqT][:, oc],
                                                                      start=False, stop=(c == 1 and hd == 3)),
                         reads=[("B", 11), ("B", bqT)], writes=[("ps", po)])
                pst = nps()
                for hd in range(4):
                    h_ = slice(hd * 128, (hd + 1) * 128)
                    P.pe(lambda e, hd=hd, h_=h_, cs=cs, pst=pst: e.matmul(psb[pst][:, h_], lhsT=Bp[bk][cs, h_],
                                                                          rhs=Bp[bv][cs, h_],
                                                                          start=(hd == 0), stop=(hd == 3)),
                         reads=[("B", bk), ("B", bv)], writes=[("ps", pst)])
                P.dve(lambda e, pst=pst: e.tensor_tensor(out=Sst[:], in0=psb[pst][:], in1=Sst[:], op=ALU.add),
                      reads=[("ps", pst), "S"], writes=["S"])
                eb4 = bass.AP(ebl[par], c, [[ebl[par].ap().ap[0][0], 128], [2, 4], [0, 128]])
                P.dve(lambda e, eb4=eb4: e.tensor_tensor(out=Sst[:].rearrange("p (h t) -> p h t", h=4),
                                                         in0=Sst[:].rearrange("p (h t) -> p h t", h=4),
                                                         in1=eb4, op=ALU.mult),
                      reads=["S", ("ebl", par)], writes=["S"])
                P.act(lambda e: e.activation(out=Bp[11][:], in_=Sst[:], func=AF.Copy),
                      reads=["S"], writes=[("B", 11)])
                if DEBUG and tl == NL - 1 and c == 0:
                    P.act(lambda e: e.activation(out=DBG_T["s1"][:], in_=Sst[:], func=AF.Copy),
                          reads=["S"], writes=["dbg_s1"])
                    P.act(lambda e: e.activation(out=DBG_T["e"][:], in_=ebl[par][:], func=AF.Copy),
                          reads=[("ebl", par)], writes=["dbg_e"])
            P.act(lambda e: e.activation(out=Bp[10][:], in_=psb[po][:], func=AF.Square),
                  reads=[("ps", po)], writes=[("B", 10)])
            pss = nps()
            P.pe(lambda e: e.matmul(psb[pss][:], lhsT=onesb[:], rhs=Bp[10][:], start=True, stop=True),
                 reads=[("B", 10), "onesb"], writes=[("ps", pss)])
            P.act(lambda e: e.activation(out=Fp[9][:], in_=psb[pss][:], func=AF.Sqrt, bias=EPS, scale=1.0 / 128.0),
                  reads=[("ps", pss)], writes=[("F", 9)])
            P.dve(lambda e: e.reciprocal(out=Fp[9][:], in_=Fp[9][:]), reads=[("F", 9)], writes=[("F", 9)])
            for hd in range(4):
                h_ = slice(hd * 128, (hd + 1) * 128)
                P.dve(lambda e, hd=hd, h_=h_: e.scalar_tensor_tensor(out=Fp[8][:, h_], in0=psb[po][:, h_],
                                                                     scalar=cc("gn", hg * 4 + hd), in1=Fp[9][:, h_],
                                                                     op0=ALU.mult, op1=ALU.mult),
                      reads=[("ps", po), ("F", 9), "cst"], writes=[("F", 8)])
            P.dve(lambda e: e.tensor_tensor(out=aT[:, hg * 4:(hg + 1) * 4, tk],
                                            in0=Fp[8][:].rearrange("p (h t) -> p h t", h=4),
                                            in1=aT[:, hg * 4:(hg + 1) * 4, tk], op=ALU.mult),
                  reads=[("F", 8)] + [atok(hg * 4 + hd, tt) for hd in range(4)],
                  writes=[atok(hg * 4 + hd, tt) for hd in range(4)])

        front(0)
        for tl in range(NL):
            if tl + 1 < NL:
                front(tl + 1)
            back(tl)

    steps = []
    outs = []
    pidx0 = [0, len(pcs[0])]
    for seq in range(NSEQ):
        def load_x(seq=seq):
            for c in range(8):
                P.dma("sp", "xin%d" % c, lambda e, c=c: e.dma_start(out=xT[:, c, :], in_=xd[seq, c * 128:(c + 1) * 128, :]),
                      writes=[xtok(c, tt) for tt in range(NT)])
        steps.append((None, lambda s, t, f=load_x: f()))
        nsub_done = 0
        for l in range(2):
            lp = pcs[l]
            gla_hold = {}
            cur_sub = None
            for j, (kind, info, _) in enumerate(lp):
                pi = pidx0[l] + j
                sub = {"conv_in": 0, "og": 0, "gla_q": 0, "gla_f": 0, "gla_i": 0, "xa_kv": 1, "xa_q": 1,
                       "ffn_in": 2}.get(kind)
                if kind == "out":
                    sub = {"conv": 0, "hgrn": 0, "xa": 1, "ffn": 2}[info[0]]
                gsub = l * 3 + sub
                if gsub >= nsub:
                    continue
                if gsub != cur_sub:
                    cur_sub = gsub
                    gname = ["g_mix%d", "g_xa%d", "g_ffn%d"][sub] % l
                    steps.append((None, lambda s, t, g=gname: norm_phase(g)))
                    if sub == 1:
                        steps.append((None, lambda s, t, seq=seq, l=l: xa_pre(seq, l)))
                if kind == "conv_in":
                    steps.append((pi, lambda s, t, m=info: conv_in(s, t, m)))
                elif kind == "out":
                    steps.append((pi, lambda s, t, info=info: out_stage(s, t, info)))
                elif kind == "ffn_in":
                    steps.append((pi, lambda s, t, lc0=info: ffn_in(s, t, lc0)))
                elif kind == "xa_kv":
                    steps.append((pi, lambda s, t, p=info: xa_kv(s, t, p)))
                elif kind == "xa_q":
                    steps.append((pi, lambda s, t, p=info: xa_q(s, t, p)))
                elif kind == "og":
                    steps.append((pi, lambda s, t, p=info: og_stage(s, t, p)))
                elif kind == "gla_q":
                    steps.append((pi, lambda s, t, gh=gla_hold: gh.update(q=(s, t))))
                elif kind == "gla_f":
                    steps.append((pi, lambda s, t, gh=gla_hold: gh.update(f=(s, t)), 1))
                elif kind == "gla_i":
                    steps.append((pi, lambda s, t, gh=gla_hold, hg=info: gla(gh["q"][0], gh["f"][0], s,
                                                                           gh["q"][1], gh["f"][1], t, hg), 2))

        def fin(seq=seq):
            for tt in range(NT):
                ts = slice(tt * 512, (tt + 1) * 512)
                if final:
                    norm_tile(lambda c, ts=ts: xT[:, c, ts], "g_fin", 512, lambda c, tt=tt: xtok(c, tt),
                              lambda c, ts=ts: xT[:, c, ts], lambda c, tt=tt: xtok(c, tt),
                              ftmp=tt % 2, btmps=[0, 1, 2, 3])
                for c in range(8):
                    outs.append(P.dma("sp", "out%d" % c, lambda e, c=c, ts=ts: e.dma_start(
                        out=od[seq, c * 128:(c + 1) * 128, ts], in_=xT[:, c, ts]),
                        reads=[xtok(c, tt)], writes=[("out", seq, c, tt)]))
        steps.append((None, lambda s, t, f=fin: f()))

    piece_steps = [i for i, st in enumerate(steps) if st[0] is not None]
    nissued = [0]

    def issue_piece(k):
        si = piece_steps[k]
        pi = steps[si][0]
        slot = slots[k % NSLOT]
        stok = ("slot", k % NSLOT)
        P.dma("pool", "w%d" % (k % NSLOT), lambda e, pi=pi, slot=slot: e.dma_start(
            out=slot[:].rearrange("p a b -> p (a b)"), in_=wpd[pi]), writes=[stok])

    kcur = 0
    for i, st in enumerate(steps):
        pi, fn = st[0], st[1]
        hold = st[2] if len(st) > 2 else 0
        if pi is not None:
            while nissued[0] < len(piece_steps) and nissued[0] < kcur - hold + NSLOT:
                issue_piece(nissued[0])
                nissued[0] += 1
            fn(slots[kcur % NSLOT], ("slot", kcur % NSLOT))
            kcur += 1
        else:
            fn(None, None)

    P.analyze(force_signal=outs)
    sems = {k: es.enter_context(nc.semaphore("s_" + "_".join(str(x) for x in k))) for k in P.semkeys}
    fw = [(("dma", "out%d" % c), P.final_counts[("dma", "out%d" % c)]) for c in range(8)]
    with nc.Block() as block:
        P.emit(sems, block, final_waits=fw)
    es.close()
    return nc, P


_CACHE = {}


def kernel(**inputs):
    inputs = {k: np.asarray(v) for k, v in inputs.items()}
    x = inputs["x"]
    B, S, _ = x.shape
    ncores = 8
    nseq = B // ncores
    if "nc" not in _CACHE:
        _CACHE["nc"] = build_program(S, nseq)[0]
    nc = _CACHE["nc"]
    wp = pack_weights(inputs)
    cst = pack_consts(inputs)
    ctab = const_tables()
    lb = np.ascontiguousarray(inputs["hgrn_lb"], dtype=np.float32)
    xTh = np.ascontiguousarray(x.transpose(0, 2, 1))
    mTh = np.ascontiguousarray(inputs["mem"].transpose(0, 2, 1))
    in_maps = []
    for c in range(ncores):
        in_maps.append({"xT": xTh[c * nseq:(c + 1) * nseq], "memT": mTh[c * nseq:(c + 1) * nseq],
                        "wp": wp, "cst": cst, "ctab": ctab, "lb": lb})
    res = run_bass_kernel_spmd(nc, in_maps, core_ids=list(range(ncores)))
    outT = np.concatenate([r["outT"] for r in res.results], axis=0)
    return np.ascontiguousarray(outT.transpose(0, 2, 1)).astype(np.float32)
```

```python
import numpy as np
from contextlib import ExitStack
import concourse.bass as bass
import concourse.mybir as mybir
from concourse.bass_utils import run_bass_kernel_spmd

F32 = mybir.dt.float32
BF16 = mybir.dt.bfloat16
ALU = mybir.AluOpType
AF = mybir.ActivationFunctionType

D = 1024
DFF = 2816
MEM = 256
EPS = 1e-6
NSLOT = 4
EPOCH = 4096
DEBUG = False
ENGINES = ("pe", "act", "dve", "pool", "sp")


class Op:
    __slots__ = ("eng", "fn", "reads", "writes", "dma", "idx", "waits", "signal", "count", "semkey")

    def __init__(self, eng, fn, reads, writes, dma):
        self.eng = eng
        self.fn = fn
        self.reads = tuple(reads)
        self.writes = tuple(writes)
        self.dma = dma
        self.waits = {}
        self.signal = False
        self.count = 0
        self.semkey = None


class Prog:
    def __init__(self):
        self.ops = []

    def add(self, eng, fn, reads=(), writes=(), dma=None):
        op = Op(eng, fn, reads, writes, dma)
        op.idx = len(self.ops)
        self.ops.append(op)
        return op

    def pe(self, fn, reads=(), writes=()):
        return self.add("pe", fn, reads, writes)

    def act(self, fn, reads=(), writes=()):
        return self.add("act", fn, reads, writes)

    def dve(self, fn, reads=(), writes=()):
        return self.add("dve", fn, reads, writes)

    def pool(self, fn, reads=(), writes=()):
        return self.add("pool", fn, reads, writes)

    def dma(self, q, stream, fn, reads=(), writes=()):
        return self.add(q, fn, reads, writes, dma=stream)

    def analyze(self, force_signal=()):
        last_writer = {}
        readers = {}
        need = [None] * len(self.ops)
        for op in self.ops:
            deps = set()
            for t in op.reads:
                w = last_writer.get(t)
                if w is not None:
                    deps.add(w)
            for t in op.writes:
                w = last_writer.get(t)
                if w is not None:
                    deps.add(w)
                deps.update(readers.get(t, ()))
            deps.discard(op.idx)
            best = {}
            nd = []
            for d in deps:
                a = self.ops[d]
                if a.dma is not None:
                    nd.append(d)
                    continue
                if op.dma is None and a.eng == op.eng and a.eng == "pe":
                    continue
                if d > best.get(a.eng, -1):
                    best[a.eng] = d
            nd.extend(best.values())
            for d in nd:
                self.ops[d].signal = True
            need[op.idx] = nd
            for t in op.writes:
                last_writer[t] = op.idx
                readers[t] = []
            for t in op.reads:
                if t not in op.writes:
                    readers.setdefault(t, []).append(op.idx)
        for op in force_signal:
            op.signal = True
        cnt = {}
        raw = {}
        for op in self.ops:
            if not op.signal:
                continue
            if op.dma is not None:
                key = ("dma", op.dma)
                cnt[key] = cnt.get(key, 0) + 16
                op.count = cnt[key]
            else:
                r = raw.get(op.eng, 0)
                key = ("eng", op.eng, r // EPOCH)
                raw[op.eng] = r + 1
                op.count = (r % EPOCH) + 1
                cnt[key] = op.count
            op.semkey = key
        self.semkeys = list(cnt.keys())
        self.final_counts = cnt
        waited = {e: {} for e in ENGINES}
        nw = 0
        for op in self.ops:
            w = {}
            for d in need[op.idx]:
                a = self.ops[d]
                if a.count > w.get(a.semkey, 0):
                    w[a.semkey] = a.count
            ww = {}
            for k, v in w.items():
                if waited[op.eng].get(k, 0) < v:
                    waited[op.eng][k] = v
                    ww[k] = v
                    nw += 1
            op.waits = ww
        self.n_waits = nw

    def emit(self, sems, block, final_waits=()):
        per = {e: [o for o in self.ops if o.eng == e] for e in ENGINES}

        def run(name):
            def body(eng):
                for op in per[name]:
                    for k, v in op.waits.items():
                        eng.wait_ge(sems[k], v)
                    ins = op.fn(eng)
                    if op.signal:
                        ins.then_inc(sems[op.semkey], 16 if op.dma is not None else 1)
                if name == "sp":
                    for k, v in final_waits:
                        eng.wait_ge(sems[k], v)
            return body

        block.tensor(run("pe"))
        block.scalar(run("act"))
        block.vector(run("dve"))
        block.gpsimd(run("pool"))
        block.sync(run("sp"))


FFN_GROUPS = ((0, 8), (8, 8), (16, 6))


def layer_pieces(l):
    out = []
    if l % 2 == 0:
        j = l // 2
        for m in range(8):
            blocks = [("conv_w_in", j, 0, 8, sec * 1024 + m * 128, 128, sec * 128) for sec in range(3)]
            out.append(("conv_in", m, blocks))
        for p in range(2):
            out.append(("out", ("conv", p, 8), [("conv_w_out", j, 0, 8, p * 512, 512, 0)]))
    else:
        j = l // 2
        for p in range(2):
            out.append(("og", p, [("hgrn_w_in", j, 0, 8, 3072 + p * 512, 512, 0)]))
        for hg in range(2):
            out.append(("gla_q", hg, [("hgrn_w_in", j, 0, 8, hg * 512, 512, 0)]))
            out.append(("gla_f", hg, [("hgrn_w_in", j, 0, 8, 1024 + hg * 512, 512, 0)]))
            out.append(("gla_i", hg, [("hgrn_w_in", j, 0, 8, 2048 + hg * 512, 512, 0)]))
        for p in range(2):
            out.append(("out", ("hgrn", p, 8), [("hgrn_w_out", j, 0, 8, p * 512, 512, 0)]))
    for p in range(4):
        out.append(("xa_kv", p, [("xattn_w_kv", l, 0, 8, p * 512, 512, 0)]))
    for p in range(2):
        out.append(("xa_q", p, [("xattn_w_q", l, 0, 8, p * 512, 512, 0)]))
    for p in range(2):
        out.append(("out", ("xa", p, 8), [("xattn_w_o", l, 0, 8, p * 512, 512, 0)]))
    for (f0, nf) in FFN_GROUPS:
        for q in range(nf // 2):
            fa, fb = f0 + 2 * q, f0 + 2 * q + 1
            blocks = [("ffn_w_in", l, 0, 8, fa * 128, 128, 0),
                      ("ffn_w_in", l, 0, 8, fb * 128, 128, 128),
                      ("ffn_w_in", l, 0, 8, DFF + fa * 128, 128, 256),
                      ("ffn_w_in", l, 0, 8, DFF + fb * 128, 128, 384)]
            out.append(("ffn_in", 2 * q, blocks))
        for p in range(2):
            out.append(("out", ("ffn", p, nf), [("ffn_w_out", l, f0 * 128, nf, p * 512, 512, 0)]))
    return out


def all_pieces():
    res = []
    for l in range(2):
        res.append(layer_pieces(l))
    return res


def pack_weights(inputs):
    pcs = all_pieces()
    n = sum(len(p) for p in pcs)
    wp = np.zeros((n, 128, 8, 512), np.float32)
    j = 0
    for lp in pcs:
        for (_, _, blocks) in lp:
            for (wn, li, r0, nkc, c0, ncol, dc) in blocks:
                w = inputs[wn][li]
                blk = w[r0:r0 + nkc * 128, c0:c0 + ncol].reshape(nkc, 128, ncol).transpose(1, 0, 2)
                wp[j, :, :nkc, dc:dc + ncol] = blk
            j += 1
    return wp.reshape(n, 128, 8 * 512)


def _fm(v):
    return np.asarray(v, np.float32).reshape(8, 128).T


CST_COLS = {}


def pack_consts(inputs):
    cols = []
    off = [0]

    def put(name, arr):
        CST_COLS[name] = (off[0], arr.shape[1])
        off[0] += arr.shape[1]
        cols.append(arr)

    for l in range(2):
        put("g_mix%d" % l, _fm(inputs["norm_mix"][l]))
        put("g_xa%d" % l, _fm(inputs["norm_xattn"][l]))
        put("g_mem%d" % l, _fm(inputs["norm_mem"][l]))
        put("g_ffn%d" % l, _fm(inputs["norm_ffn"][l]))
    put("g_fin", _fm(inputs["final_norm"]))
    put("gn", _fm(inputs["hgrn_norm"][0]))
    for jj in range(3):
        put("cw%d" % jj, _fm(inputs["conv_w"][0][jj]))
    return np.ascontiguousarray(np.concatenate(cols, axis=1))


def const_tables():
    s = np.arange(128)
    same = (s[:, None] // 64) == (s[None, :] // 64)
    tri = (same & (s[:, None] <= s[None, :])).astype(np.float32)
    ident = np.eye(128, dtype=np.float32)
    cm = np.zeros((128, 2), np.float32)
    cm[:64, 0] = 1.0
    cm[64:, 1] = 1.0
    return np.ascontiguousarray(np.concatenate([ident, tri, cm], axis=1))


def build_program(S, NSEQ, nsub=6, final=True):
    NT = S // 512
    NL = S // 128
    pcs = all_pieces()
    npieces = sum(len(p) for p in pcs)
    dummy = {"norm_mix": np.zeros((2, D)), "norm_xattn": np.zeros((2, D)), "norm_mem": np.zeros((2, D)),
             "norm_ffn": np.zeros((2, D)), "final_norm": np.zeros(D), "hgrn_norm": np.zeros((1, D)),
             "conv_w": np.zeros((1, 3, D))}
    ncst = pack_consts(dummy).shape[1]

    nc = bass.Bass("TRN2", target_bir_lowering=False)
    xd = nc.dram_tensor("xT", [NSEQ, D, S], F32, kind="ExternalInput").ap()
    md = nc.dram_tensor("memT", [NSEQ, D, MEM], F32, kind="ExternalInput").ap()
    wpd = nc.dram_tensor("wp", [npieces, 128, 8 * 512], F32, kind="ExternalInput").ap()
    cstd = nc.dram_tensor("cst", [128, ncst], F32, kind="ExternalInput").ap()
    ctabd = nc.dram_tensor("ctab", [128, 258], F32, kind="ExternalInput").ap()
    lbd = nc.dram_tensor("lb", [2, D], F32, kind="ExternalInput")
    od = nc.dram_tensor("outT", [NSEQ, D, S], F32, kind="ExternalOutput").ap()

    P = Prog()
    es = ExitStack()
    sb = lambda name, shape, dt: es.enter_context(nc.sbuf_tensor(name, shape, dt))
    xT = sb("xT_sb", [128, 8, S], F32)
    hT = sb("hT_sb", [128, 8, S], BF16)
    aT = sb("aT_sb", [128, 8, S], BF16)
    slots = [sb("slot%d" % i, [128, 8, 512], BF16) for i in range(NSLOT)]
    NF, NB = 10, 12
    Fp = [sb("F%d" % i, [128, 512], F32) for i in range(NF)]
    Bp = [sb("B%d" % i, [128, 512], BF16) for i in range(NB)]
    cst = sb("cst_sb", [128, ncst], F32)
    ctab = sb("ctab_sb", [128, 258], F32)
    identb = sb("identb", [128, 128], BF16)
    onesb = sb("onesb", [128, 128], BF16)
    maskb = sb("maskb", [128, 128], BF16)
    oml = sb("oml", [128, D], F32)
    Sst = sb("Sst", [128, 512], F32)
    vcar = [sb("vcar%d" % i, [128, 514], F32) for i in range(2)]
    ebl = [sb("ebl%d" % i, [128, 8], F32) for i in range(2)]
    psb = [es.enter_context(nc.psum_tensor("ps%d" % i, [128, 512], F32)) for i in range(8)]
    ps_state = {"n": 0}
    DBG_T = {}
    if DEBUG:
        DBG_T["s1"] = sb("dbg_s1", [128, 512], F32)
        DBG_T["e"] = sb("dbg_e", [128, 8], F32)

    ps_state["ring"] = list(range(8))

    def nps():
        r = ps_state["ring"]
        b = r[ps_state["n"] % len(r)]
        ps_state["n"] += 1
        return b

    def cc(name, c=None):
        o, n = CST_COLS[name]
        if c is None:
            return cst[:, o:o + n]
        return cst[:, o + c:o + c + 1]

    tri32 = ctab[:, 128:256]
    cm32 = ctab[:, 256:258]

    P.dma("sp", "cst0", lambda e: e.dma_start(out=cst[:], in_=cstd), writes=["cst"])
    P.dma("sp", "cst1", lambda e: e.dma_start(out=ctab[:], in_=ctabd), writes=["ctab"])
    lb0 = bass.AP(lbd, 0, [[0, 128], [1, D]])
    P.dma("sp", "cst2", lambda e: e.dma_start(out=oml[:], in_=lb0), writes=["oml"])
    for hf in range(2):
        lb1 = bass.AP(lbd, D + hf * 512, [[0, 128], [1, 512]])
        P.dma("sp", "cst3_%d" % hf, lambda e, hf=hf, lb1=lb1: e.dma_start(out=Fp[4 + hf][:], in_=lb1), writes=[("F", 4 + hf)])
    P.dve(lambda e: e.tensor_copy(out=identb[:], in_=ctab[:, 0:128]), reads=["ctab"], writes=["identb"])
    P.dve(lambda e: e.tensor_copy(out=maskb[:], in_=ctab[:, 128:256]), reads=["ctab"], writes=["maskb"])
    P.dve(lambda e: e.memset(onesb[:], 1.0), writes=["onesb"])
    for hf in range(2):
        P.dve(lambda e, hf=hf: e.tensor_tensor(out=oml[:, hf * 512:(hf + 1) * 512], in0=oml[:, hf * 512:(hf + 1) * 512],
                                               in1=Fp[4 + hf][:], op=ALU.subtract),
              reads=["oml", ("F", 4 + hf)], writes=["oml"])
    P.act(lambda e: e.activation(out=oml[:], in_=oml[:], func=AF.Sigmoid), reads=["oml"], writes=["oml"])

    def xtok(c, tt):
        return ("x", c, tt)

    def htok(tt):
        return ("h", tt)

    def atok(c, tt):
        return ("a", c, tt)

    def norm_tile(src_fn, g_name, N, reads, out_fn, out_writes, ftmp, btmps):
        pb = nps()
        for c in range(8):
            bi = btmps[c % len(btmps)]
            P.act(lambda e, c=c, bi=bi: e.activation(out=Bp[bi][:, 0:N], in_=src_fn(c), func=AF.Square),
                  reads=[reads(c)], writes=[("B", bi)])
            P.pe(lambda e, c=c, bi=bi, pb=pb: e.matmul(psb[pb][:, 0:N], lhsT=onesb[:], rhs=Bp[bi][:, 0:N],
                                                     start=(c == 0), stop=(c == 7)),
                 reads=[("B", bi), "onesb"], writes=[("ps", pb)])
        P.act(lambda e, pb=pb: e.activation(out=Fp[ftmp][:, 0:N], in_=psb[pb][:, 0:N], func=AF.Sqrt,
                                            bias=EPS, scale=1.0 / D),
              reads=[("ps", pb)], writes=[("F", ftmp)])
        P.dve(lambda e: e.reciprocal(out=Fp[ftmp][:, 0:N], in_=Fp[ftmp][:, 0:N]),
              reads=[("F", ftmp)], writes=[("F", ftmp)])
        for c in range(8):
            P.dve(lambda e, c=c: e.scalar_tensor_tensor(out=out_fn(c), in0=src_fn(c), scalar=cc(g_name, c),
                                                        in1=Fp[ftmp][:, 0:N], op0=ALU.mult, op1=ALU.mult),
                  reads=[reads(c), ("F", ftmp), "cst"], writes=[out_writes(c)])

    def norm_phase(g_name):
        for tt in range(NT):
            ts = slice(tt * 512, (tt + 1) * 512)
            norm_tile(lambda c, ts=ts: xT[:, c, ts], g_name, 512, lambda c, tt=tt: xtok(c, tt),
                      lambda c, ts=ts: hT[:, c, ts], lambda c, tt=tt: htok(tt),
                      ftmp=tt % 2, btmps=[0, 1, 2, 3])

    def out_stage(slot, stok, info, after_tile=None):
        _, p, nkc = info
        for tt in range(NT):
            ts = slice(tt * 512, (tt + 1) * 512)
            for mo in range(4):
                pb = nps()
                ca = p * 4 + mo
                for kc in range(nkc):
                    P.pe(lambda e, kc=kc, mo=mo, pb=pb, ts=ts: e.matmul(
                        psb[pb][:], lhsT=slot[:, kc, mo * 128:(mo + 1) * 128], rhs=aT[:, kc, ts],
                        start=(kc == 0), stop=(kc == nkc - 1)),
                        reads=[stok, atok(kc, tt)], writes=[("ps", pb)])
                P.dve(lambda e, ca=ca, pb=pb, ts=ts: e.tensor_tensor(out=xT[:, ca, ts], in0=psb[pb][:],
                                                                     in1=xT[:, ca, ts], op=ALU.add),
                      reads=[("ps", pb), xtok(ca, tt)], writes=[xtok(ca, tt)])
            if after_tile is not None and tt >= 1:
                after_tile(tt - 1)
        if after_tile is not None:
            after_tile(NT - 1)

    def norm_one(g_name, tt):
        ts = slice(tt * 512, (tt + 1) * 512)
        norm_tile(lambda c, ts=ts: xT[:, c, ts], g_name, 512, lambda c, tt=tt: xtok(c, tt),
                  lambda c, ts=ts: hT[:, c, ts], lambda c, tt=tt: htok(tt),
                  ftmp=tt % 2, btmps=[0, 1, 2, 3])

    def conv_in(slot, stok, m):
        for tt in range(NT):
            ts = slice(tt * 512, (tt + 1) * 512)
            vb, vn = vcar[tt % 2], vcar[(tt + 1) % 2]
            vtk, vntk = ("vcar", tt % 2), ("vcar", (tt + 1) % 2)
            if tt == 0:
                P.dve(lambda e, vb=vb: e.memset(vb[:, 0:2], 0.0), writes=[vtk])
            pbs = []
            for sec in range(3):
                pb = nps()
                pbs.append(pb)
                for kc in range(8):
                    P.pe(lambda e, kc=kc, sec=sec, pb=pb, ts=ts: e.matmul(
                        psb[pb][:], lhsT=slot[:, kc, sec * 128:(sec + 1) * 128], rhs=hT[:, kc, ts],
                        start=(kc == 0), stop=(kc == 7)),
                        reads=[stok, htok(tt)], writes=[("ps", pb)])
            pgb, pgc, pu = pbs
            fu, fz = 2 + (tt % 2), 4 + (tt % 2)
            P.act(lambda e, pu=pu, fu=fu: e.activation(out=Fp[fu][:], in_=psb[pu][:], func=AF.Copy),
                  reads=[("ps", pu)], writes=[("F", fu)])
            P.dve(lambda e, pgc=pgc, fu=fu, vb=vb: e.tensor_tensor(out=vb[:, 2:514], in0=psb[pgc][:], in1=Fp[fu][:],
                                                                   op=ALU.mult),
                  reads=[("ps", pgc), ("F", fu)], writes=[vtk])
            if tt + 1 < NT:
                P.act(lambda e, vb=vb, vn=vn: e.activation(out=vn[:, 0:2], in_=vb[:, 512:514], func=AF.Copy),
                      reads=[vtk], writes=[vntk])
            P.act(lambda e, vb=vb, fz=fz, m=m: e.activation(out=Fp[fz][:], in_=vb[:, 2:514], func=AF.Copy,
                                                            scale=cc("cw2", m)),
                  reads=[vtk, "cst"], writes=[("F", fz)])
            P.dve(lambda e, vb=vb, fz=fz, m=m: e.scalar_tensor_tensor(out=Fp[fz][:], in0=vb[:, 1:513],
                                                                      scalar=cc("cw1", m), in1=Fp[fz][:],
                                                                      op0=ALU.mult, op1=ALU.add),
                  reads=[vtk, ("F", fz), "cst"], writes=[("F", fz)])
            P.dve(lambda e, vb=vb, fz=fz, m=m: e.scalar_tensor_tensor(out=Fp[fz][:], in0=vb[:, 0:512],
                                                                      scalar=cc("cw0", m), in1=Fp[fz][:],
                                                                      op0=ALU.mult, op1=ALU.add),
                  reads=[vtk, ("F", fz), "cst"], writes=[("F", fz)])
            P.dve(lambda e, pgb=pgb, fz=fz, m=m, ts=ts: e.tensor_tensor(out=aT[:, m, ts], in0=psb[pgb][:],
                                                                        in1=Fp[fz][:], op=ALU.mult),
                  reads=[("ps", pgb), ("F", fz)], writes=[atok(m, tt)])

    def ffn_in(slot, stok, lc0):
        for tt in range(NT):
            ts = slice(tt * 512, (tt + 1) * 512)
            for l2 in range(2):
                pa, pbk = nps(), nps()
                for (pb, co) in ((pa, l2 * 128), (pbk, 256 + l2 * 128)):
                    for kc in range(8):
                        P.pe(lambda e, kc=kc, pb=pb, co=co, ts=ts: e.matmul(
                            psb[pb][:], lhsT=slot[:, kc, co:co + 128], rhs=hT[:, kc, ts],
                            start=(kc == 0), stop=(kc == 7)),
                            reads=[stok, htok(tt)], writes=[("ps", pb)])
                fs = 6 + ((2 * tt + l2) % 2)
                lc = lc0 + l2
                P.act(lambda e, pa=pa, fs=fs: e.activation(out=Fp[fs][:], in_=psb[pa][:], func=AF.Silu),
                      reads=[("ps", pa)], writes=[("F", fs)])
                P.dve(lambda e, pbk=pbk, fs=fs, lc=lc, ts=ts: e.tensor_tensor(out=aT[:, lc, ts], in0=psb[pbk][:],
                                                                              in1=Fp[fs][:], op=ALU.mult),
                      reads=[("ps", pbk), ("F", fs)], writes=[atok(lc, tt)])

    def bview(i0, n, inner):
        raise NotImplementedError

    def memn_ap(c):
        return Bp[c // 2][:, (c % 2) * 256:(c % 2) * 256 + 256]

    def memn_tok(c):
        return ("B", c // 2)

    def kT_ap(dch):
        return Bp[4 + dch // 2][:, (dch % 2) * 256:(dch % 2) * 256 + 256]

    def kT_tok(dch):
        return ("B", 4 + dch // 2)

    def v_ap(mc, col0, n):
        half = col0 // 512
        return Bp[8 + mc * 2 + half][:, col0 % 512:col0 % 512 + n]

    def v_tok(mc, col0):
        return ("B", 8 + mc * 2 + col0 // 512)

    def xa_pre(seq, l):
        def mT(c):
            return Fp[c // 2][:, (c % 2) * 256:(c % 2) * 256 + 256]
        for c in range(8):
            P.dma("sp", "mem%d" % c, lambda e, c=c: e.dma_start(out=mT(c), in_=md[seq, c * 128:(c + 1) * 128, :]),
                  writes=[("F", c // 2)])
        norm_tile(mT, "g_mem%d" % l, MEM, lambda c: ("F", c // 2),
                  memn_ap, memn_tok, ftmp=8, btmps=[8, 9, 10, 11])

    def xa_kv(slot, stok, p):
        if p < 2:
            for j in range(4):
                dch = p * 4 + j
                pb = nps()
                for kc in range(8):
                    P.pe(lambda e, kc=kc, j=j, pb=pb: e.matmul(psb[pb][:, 0:MEM], lhsT=slot[:, kc, j * 128:(j + 1) * 128],
                                                             rhs=memn_ap(kc), start=(kc == 0), stop=(kc == 7)),
                         reads=[stok, memn_tok(kc)], writes=[("ps", pb)])
                P.act(lambda e, pb=pb, dch=dch: e.activation(out=kT_ap(dch), in_=psb[pb][:, 0:MEM], func=AF.Copy),
                      reads=[("ps", pb)], writes=[kT_tok(dch)])
        else:
            col0 = (p - 2) * 512
            for mc in range(2):
                pb = nps()
                for kc in range(8):
                    P.pe(lambda e, kc=kc, mc=mc, pb=pb: e.matmul(
                        psb[pb][:], lhsT=memn_ap(kc)[:, mc * 128:(mc + 1) * 128], rhs=slot[:, kc, :],
                        start=(kc == 0), stop=(kc == 7)),
                        reads=[stok, memn_tok(kc)], writes=[("ps", pb)])
                P.act(lambda e, pb=pb, mc=mc, col0=col0: e.activation(out=v_ap(mc, col0, 512), in_=psb[pb][:],
                                                                      func=AF.Copy),
                      reads=[("ps", pb)], writes=[v_tok(mc, col0)])

    def xa_q(slot, stok, p):
        for tt in range(NT):
            ts = slice(tt * 512, (tt + 1) * 512)
            for j in range(4):
                pb = nps()
                for kc in range(8):
                    P.pe(lambda e, kc=kc, j=j, pb=pb, ts=ts: e.matmul(
                        psb[pb][:], lhsT=slot[:, kc, j * 128:(j + 1) * 128], rhs=hT[:, kc, ts],
                        start=(kc == 0), stop=(kc == 7)),
                        reads=[stok, htok(tt)], writes=[("ps", pb)])
                P.act(lambda e, pb=pb, j=j: e.activation(out=Bp[j][:], in_=psb[pb][:], func=AF.Copy),
                      reads=[("ps", pb)], writes=[("B", j)])
            for hh in range(2):
                head = 2 * p + hh
                fpt = 4 + hh
                PT = Fp[fpt][:].bitcast(BF16)
                for mc in range(2):
                    pb = nps()
                    for jj in range(2):
                        dabs = head * 2 + jj
                        P.pe(lambda e, jj=jj, mc=mc, pb=pb, dabs=dabs, hh=hh: e.matmul(
                            psb[pb][:], lhsT=kT_ap(dabs)[:, mc * 128:(mc + 1) * 128], rhs=Bp[2 * hh + jj][:],
                            start=(jj == 0), stop=(jj == 1)),
                            reads=[kT_tok(dabs), ("B", 2 * hh + jj)], writes=[("ps", pb)])
                    P.act(lambda e, pb=pb, mc=mc, PT=PT: e.activation(out=PT[:, mc * 512:(mc + 1) * 512], in_=psb[pb][:],
                                                                      func=AF.Exp, scale=1.0 / 16.0),
                          reads=[("ps", pb)], writes=[("F", fpt)])
                pd = nps()
                for mc in range(2):
                    P.pe(lambda e, mc=mc, pd=pd, PT=PT: e.matmul(psb[pd][:], lhsT=onesb[:],
                                                                 rhs=PT[:, mc * 512:(mc + 1) * 512],
                                                                 start=(mc == 0), stop=(mc == 1)),
                         reads=[("F", fpt), "onesb"], writes=[("ps", pd)])
                frd = 6 + hh
                P.dve(lambda e, pd=pd, frd=frd: e.reciprocal(out=Fp[frd][:], in_=psb[pd][:]),
                      reads=[("ps", pd)], writes=[("F", frd)])
                for jj in range(2):
                    dabs = head * 2 + jj
                    po = nps()
                    for mc in range(2):
                        P.pe(lambda e, mc=mc, po=po, PT=PT, head=head, jj=jj: e.matmul(
                            psb[po][:], lhsT=v_ap(mc, head * 256 + jj * 128, 128),
                            rhs=PT[:, mc * 512:(mc + 1) * 512], start=(mc == 0), stop=(mc == 1)),
                            reads=[("F", fpt), v_tok(mc, head * 256 + jj * 128)], writes=[("ps", po)])
                    P.dve(lambda e, po=po, frd=frd, dabs=dabs, ts=ts: e.tensor_tensor(
                        out=aT[:, dabs, ts], in0=psb[po][:], in1=Fp[frd][:], op=ALU.mult),
                        reads=[("ps", po), ("F", frd)], writes=[atok(dabs, tt)])

    def og_stage(slot, stok, p):
        for tt in range(NT):
            ts = slice(tt * 512, (tt + 1) * 512)
            for j in range(4):
                pb = nps()
                ca = p * 4 + j
                for kc in range(8):
                    P.pe(lambda e, kc=kc, j=j, pb=pb, ts=ts: e.matmul(
                        psb[pb][:], lhsT=slot[:, kc, j * 128:(j + 1) * 128], rhs=hT[:, kc, ts],
                        start=(kc == 0), stop=(kc == 7)),
                        reads=[stok, htok(tt)], writes=[("ps", pb)])
                P.act(lambda e, pb=pb, ca=ca, ts=ts: e.activation(out=aT[:, ca, ts], in_=psb[pb][:], func=AF.Silu),
                      reads=[("ps", pb)], writes=[atok(ca, tt)])

    def gla(sq_, sf_, si_, tq, tf, ti, hg):
        hc = slice(hg * 512, (hg + 1) * 512)
        FQ, FK, FL = (0, 4), (1, 5), (2, 6)
        P.dve(lambda e: e.memset(Sst[:], 0.0), writes=["S"])
        P.dve(lambda e: e.memset(Bp[11][:], 0.0), writes=[("B", 11)])
        ps_state["ring"] = list(range(6))
        st = {}

        def front_A(tl):
            par = tl % 2
            tk = slice(tl * 128, (tl + 1) * 128)
            tt = tl // 4
            pq, pf, pi_ = nps(), nps(), nps()
            for (pb, sl, stk) in ((pq, sq_, tq), (pf, sf_, tf), (pi_, si_, ti)):
                for kc in range(8):
                    P.pe(lambda e, kc=kc, pb=pb, sl=sl: e.matmul(psb[pb][:], lhsT=hT[:, kc, tk], rhs=sl[:, kc, :],
                                                               start=(kc == 0), stop=(kc == 7)),
                         reads=[stk, htok(tt)], writes=[("ps", pb)])
            fq, fk, fl, bv = FQ[par], FK[par], FL[par], 3 + par
            P.act(lambda e: e.activation(out=Fp[fq][:], in_=psb[pq][:], func=AF.Silu),
                  reads=[("ps", pq)], writes=[("F", fq)])
            P.act(lambda e: e.activation(out=Fp[fk][:], in_=psb[pf][:], func=AF.Sigmoid, scale=-1.0),
                  reads=[("ps", pf)], writes=[("F", fk)])
            P.act(lambda e: e.activation(out=Bp[bv][:], in_=psb[pi_][:], func=AF.Copy),
                  reads=[("ps", pi_)], writes=[("B", bv)])
            P.dve(lambda e: e.tensor_tensor(out=Fp[fk][:], in0=Fp[fk][:], in1=oml[:, hc], op=ALU.mult),
                  reads=[("F", fk), "oml"], writes=[("F", fk)])
            P.act(lambda e: e.activation(out=Fp[fl][:], in_=Fp[fk][:], func=AF.Ln, scale=-1.0, bias=1.0),
                  reads=[("F", fk)], writes=[("F", fl)])

        def front_BC(tl):
            par = tl % 2
            fq, fk, fl = FQ[par], FK[par], FL[par]
            bk, bq, bkT, bqT = 0 + par, 2, 5 + par, 7 + par
            pbb = nps()
            P.pe(lambda e: e.matmul(psb[pbb][:], lhsT=tri32, rhs=Fp[fl][:], start=True, stop=True),
                 reads=[("F", fl), "ctab"], writes=[("ps", pbb)])
            pbl = nps()
            for hd in range(4):
                P.pe(lambda e, hd=hd: e.matmul(psb[pbl][:, 2 * hd:2 * hd + 2], lhsT=Fp[fl][:, hd * 128:(hd + 1) * 128],
                                               rhs=cm32, start=(hd == 0), stop=(hd == 3)),
                     reads=[("F", fl), "ctab"], writes=[("ps", pbl)])
            P.act(lambda e: e.activation(out=Fp[3][:], in_=psb[pbb][:], func=AF.Exp, scale=-1.0),
                  reads=[("ps", pbb)], writes=[("F", 3)])
            P.dve(lambda e: e.tensor_tensor(out=Bp[bk][:], in0=Fp[fk][:], in1=Fp[3][:], op=ALU.mult),
                  reads=[("F", fk), ("F", 3)], writes=[("B", bk)])
            P.act(lambda e: e.activation(out=Fp[3][:], in_=psb[pbb][:], func=AF.Exp),
                  reads=[("ps", pbb)], writes=[("F", 3)])
            P.dve(lambda e: e.tensor_tensor(out=Bp[bq][:], in0=Fp[fq][:], in1=Fp[3][:], op=ALU.mult),
                  reads=[("F", fq), ("F", 3)], writes=[("B", bq)])
            P.act(lambda e: e.activation(out=ebl[par][:], in_=psb[pbl][:, 0:8], func=AF.Exp),
                  reads=[("ps", pbl)], writes=[("ebl", par)])
            for (src, dst) in ((bk, bkT), (bq, bqT)):
                pT = nps()
                pv = psb[pT][:].bitcast(BF16)
                for hd in range(4):
                    P.pe(lambda e, hd=hd, src=src, pv=pv: e.transpose(pv[:, hd * 128:(hd + 1) * 128],
                                                                      Bp[src][:, hd * 128:(hd + 1) * 128], identb[:]),
                         reads=[("B", src), "identb"], writes=[("ps", pT)])
                if dst == bkT:
                    P.act(lambda e, pv=pv, dst=dst: e.activation(out=Bp[dst][:], in_=pv[:, 0:512], func=AF.Copy),
                          reads=[("ps", pT)], writes=[("B", dst)])
                else:
                    P.dve(lambda e, pv=pv, dst=dst: e.tensor_copy(out=Bp[dst][:], in_=pv[:, 0:512]),
                          reads=[("ps", pT)], writes=[("B", dst)])

        def s_update(par, c, pst):
            P.dve(lambda e: e.tensor_tensor(out=Sst[:], in0=psb[pst][:], in1=Sst[:], op=ALU.add),
                  reads=[("ps", pst), "S"], writes=["S"])
            eb4 = bass.AP(ebl[par], c, [[ebl[par].ap().ap[0][0], 128], [2, 4], [0, 128]])
            P.dve(lambda e: e.tensor_tensor(out=Sst[:].rearrange("p (h t) -> p h t", h=4),
                                            in0=Sst[:].rearrange("p (h t) -> p h t", h=4),
                                            in1=eb4, op=ALU.mult),
                  reads=["S", ("ebl", par)], writes=["S"])
            P.act(lambda e: e.activation(out=Bp[11][:], in_=Sst[:], func=AF.Copy),
                  reads=["S"], writes=[("B", 11)])

        def inter(par, c, po):
            bqT = 7 + par
            for hd in range(4):
                h_ = slice(hd * 128, (hd + 1) * 128)
                oc = slice(hd * 128 + c * 64, hd * 128 + (c + 1) * 64)
                P.pe(lambda e, hd=hd, h_=h_, oc=oc: e.matmul(psb[po][:, oc], lhsT=Bp[11][:, h_], rhs=Bp[bqT][:, oc],
                                                             start=False, stop=(c == 1 and hd == 3)),
                     reads=[("B", 11), ("B", bqT)], writes=[("ps", po)])

        def back_1(tl):
            par = tl % 2
            bk, bv, bkT, bqT = 0 + par, 3 + par, 5 + par, 7 + par
            pat = nps()
            for hd in range(4):
                h_ = slice(hd * 128, (hd + 1) * 128)
                P.pe(lambda e, hd=hd, h_=h_: e.matmul(psb[pat][:, h_], lhsT=Bp[bkT][:, h_], rhs=Bp[bqT][:, h_],
                                                      start=(hd == 0), stop=(hd == 3)),
                     reads=[("B", bkT), ("B", bqT)], writes=[("ps", pat)])
            mask4 = maskb[:, :].unsqueeze(1).to_broadcast([128, 4, 128])
            P.dve(lambda e: e.tensor_tensor(out=Bp[9][:].rearrange("p (h t) -> p h t", h=4),
                                            in0=psb[pat][:].rearrange("p (h t) -> p h t", h=4),
                                            in1=mask4, op=ALU.mult),
                  reads=[("ps", pat), "maskb"], writes=[("B", 9)])
            psts = []
            for c in range(2):
                cs = slice(c * 64, (c + 1) * 64)
                pst = nps()
                psts.append(pst)
                for hd in range(4):
                    h_ = slice(hd * 128, (hd + 1) * 128)
                    P.pe(lambda e, hd=hd, h_=h_, cs=cs, pst=pst: e.matmul(psb[pst][:, h_], lhsT=Bp[bk][cs, h_],
                                                                          rhs=Bp[bv][cs, h_],
                                                                          start=(hd == 0), stop=(hd == 3)),
                         reads=[("B", bk), ("B", bv)], writes=[("ps", pst)])
            po = 6 + par
            for hd in range(4):
                h_ = slice(hd * 128, (hd + 1) * 128)
                P.pe(lambda e, hd=hd, h_=h_: e.matmul(psb[po][:, h_], lhsT=Bp[bv][:, h_], rhs=Bp[9][:, h_],
                                                      start=(hd == 0), stop=False),
                     reads=[("B", bv), ("B", 9)], writes=[("ps", po)])
            inter(par, 0, po)
            s_update(par, 0, psts[0])
            st[tl] = (po, psts[1])

        def back_2(tl):
            par = tl % 2
            po, pst1 = st[tl]
            inter(par, 1, po)
            s_update(par, 1, pst1)
            P.act(lambda e: e.activation(out=Bp[10][:], in_=psb[po][:], func=AF.Square),
                  reads=[("ps", po)], writes=[("B", 10)])

        def back_3(tl):
            tk = slice(tl * 128, (tl + 1) * 128)
            tt = tl // 4
            po, _ = st[tl]
            pss = nps()
            P.pe(lambda e: e.matmul(psb[pss][:], lhsT=onesb[:], rhs=Bp[10][:], start=True, stop=True),
                 reads=[("B", 10), "onesb"], writes=[("ps", pss)])
            P.act(lambda e: e.activation(out=Fp[9][:], in_=psb[pss][:], func=AF.Sqrt, bias=EPS, scale=1.0 / 128.0),
                  reads=[("ps", pss)], writes=[("F", 9)])
            P.dve(lambda e: e.reciprocal(out=Fp[9][:], in_=Fp[9][:]), reads=[("F", 9)], writes=[("F", 9)])
            for hd in range(4):
                h_ = slice(hd * 128, (hd + 1) * 128)
                P.dve(lambda e, hd=hd, h_=h_: e.scalar_tensor_tensor(out=Fp[8][:, h_], in0=psb[po][:, h_],
                                                                     scalar=cc("gn", hg * 4 + hd), in1=Fp[9][:, h_],
                                                                     op0=ALU.mult, op1=ALU.mult),
                      reads=[("ps", po), ("F", 9), "cst"], writes=[("F", 8)])
            P.dve(lambda e: e.tensor_tensor(out=aT[:, hg * 4:(hg + 1) * 4, tk],
                                            in0=Fp[8][:].rearrange("p (h t) -> p h t", h=4),
                                            in1=aT[:, hg * 4:(hg + 1) * 4, tk], op=ALU.mult),
                  reads=[("F", 8)] + [atok(hg * 4 + hd, tt) for hd in range(4)],
                  writes=[atok(hg * 4 + hd, tt) for hd in range(4)])

        front_A(0)
        front_BC(0)
        if NL > 1:
            front_A(1)
        for tl in range(NL):
            back_1(tl)
            if tl + 2 < NL:
                front_A(tl + 2)
            back_2(tl)
            if tl + 1 < NL:
                front_BC(tl + 1)
            back_3(tl)
        ps_state["ring"] = list(range(8))

    steps = []
    outs = []
    pidx0 = [0, len(pcs[0])]
    SUBOF = {"conv_in": 0, "og": 0, "gla_q": 0, "gla_f": 0, "gla_i": 0, "xa_kv": 1, "xa_q": 1, "ffn_in": 2}
    for seq in range(NSEQ):
        def load_x(seq=seq):
            for c in range(8):
                P.dma("sp", "xin%d" % c, lambda e, c=c: e.dma_start(out=xT[:, c, :], in_=xd[seq, c * 128:(c + 1) * 128, :]),
                      writes=[xtok(c, tt) for tt in range(NT)])
        steps.append((None, lambda s, t, f=load_x: f()))

        def fin_tile(tt, seq=seq):
            ts = slice(tt * 512, (tt + 1) * 512)
            if final:
                norm_tile(lambda c, ts=ts: xT[:, c, ts], "g_fin", 512, lambda c, tt=tt: xtok(c, tt),
                          lambda c, ts=ts: xT[:, c, ts], lambda c, tt=tt: xtok(c, tt),
                          ftmp=tt % 2, btmps=[0, 1, 2, 3])
            for c in range(8):
                outs.append(P.dma("sp", "out%d" % c, lambda e, c=c, ts=ts: e.dma_start(
                    out=od[seq, c * 128:(c + 1) * 128, ts], in_=xT[:, c, ts]),
                    reads=[xtok(c, tt)], writes=[("out", seq, c, tt)]))

        subs = []
        for l in range(2):
            for j, (kind, info, _) in enumerate(pcs[l]):
                sub = SUBOF.get(kind)
                if kind == "out":
                    sub = {"conv": 0, "hgrn": 0, "xa": 1, "ffn": 2}[info[0]]
                gsub = l * 3 + sub
                if gsub >= nsub:
                    continue
                if not subs or subs[-1][0] != gsub:
                    subs.append((gsub, l, sub, []))
                subs[-1][3].append((pidx0[l] + j, kind, info))

        def gname_of(k):
            _, l, sub, _ = subs[k]
            return ["g_mix%d", "g_xa%d", "g_ffn%d"][sub] % l

        if not subs:
            steps.append((None, lambda s, t, f=fin_tile: [f(tt) for tt in range(NT)]))
        for k, (gsub, l, sub, plist) in enumerate(subs):
            if k == 0:
                steps.append((None, lambda s, t, g=gname_of(0): norm_phase(g)))
            if sub == 1:
                steps.append((None, lambda s, t, seq=seq, l=l: xa_pre(seq, l)))
            gla_hold = {}
            for ip, (pi, kind, info) in enumerate(plist):
                last = (ip == len(plist) - 1)
                if kind == "conv_in":
                    steps.append((pi, lambda s, t, m=info: conv_in(s, t, m)))
                elif kind == "out":
                    if last:
                        if k + 1 < len(subs):
                            hook = (lambda tt, g=gname_of(k + 1): norm_one(g, tt))
                        else:
                            hook = fin_tile
                        steps.append((pi, lambda s, t, info=info, hook=hook: out_stage(s, t, info, hook)))
                    else:
                        steps.append((pi, lambda s, t, info=info: out_stage(s, t, info)))
                elif kind == "ffn_in":
                    steps.append((pi, lambda s, t, lc0=info: ffn_in(s, t, lc0)))
                elif kind == "xa_kv":
                    steps.append((pi, lambda s, t, p=info: xa_kv(s, t, p)))
                elif kind == "xa_q":
                    steps.append((pi, lambda s, t, p=info: xa_q(s, t, p)))
                elif kind == "og":
                    steps.append((pi, lambda s, t, p=info: og_stage(s, t, p)))
                elif kind == "gla_q":
                    steps.append((pi, lambda s, t, gh=gla_hold: gh.update(q=(s, t))))
                elif kind == "gla_f":
                    steps.append((pi, lambda s, t, gh=gla_hold: gh.update(f=(s, t)), 1))
                elif kind == "gla_i":
                    steps.append((pi, lambda s, t, gh=gla_hold, hg=info: gla(gh["q"][0], gh["f"][0], s,
                                                                           gh["q"][1], gh["f"][1], t, hg), 2))

    piece_steps = [i for i, st in enumerate(steps) if st[0] is not None]
    nissued = [0]

    def issue_piece(k):
        si = piece_steps[k]
        pi = steps[si][0]
        slot = slots[k % NSLOT]
        stok = ("slot", k % NSLOT)
        P.dma("pool", "w%d" % (k % NSLOT), lambda e, pi=pi, slot=slot: e.dma_start(
            out=slot[:].rearrange("p a b -> p (a b)"), in_=wpd[pi]), writes=[stok])

    kcur = 0
    for i, st in enumerate(steps):
        pi, fn = st[0], st[1]
        hold = st[2] if len(st) > 2 else 0
        if pi is not None:
            while nissued[0] < len(piece_steps) and nissued[0] < kcur - hold + NSLOT:
                issue_piece(nissued[0])
                nissued[0] += 1
            fn(slots[kcur % NSLOT], ("slot", kcur % NSLOT))
            kcur += 1
        else:
            fn(None, None)

    P.analyze(force_signal=outs)
    sems = {k: es.enter_context(nc.semaphore("s_" + "_".join(str(x) for x in k))) for k in P.semkeys}
    fw = [(("dma", "out%d" % c), P.final_counts[("dma", "out%d" % c)]) for c in range(8)]
    with nc.Block() as block:
        P.emit(sems, block, final_waits=fw)
    es.close()
    return nc, P


_CACHE = {}


def kernel(**inputs):
    inputs = {k: np.asarray(v) for k, v in inputs.items()}
    x = inputs["x"]
    B, S, _ = x.shape
    ncores = 8
    nseq = B // ncores
    if "nc" not in _CACHE:
        _CACHE["nc"] = build_program(S, nseq)[0]
    nc = _CACHE["nc"]
    wp = pack_weights(inputs)
    cst = pack_consts(inputs)
    ctab = const_tables()
    lb = np.ascontiguousarray(inputs["hgrn_lb"], dtype=np.float32)
    xTh = np.ascontiguousarray(x.transpose(0, 2, 1))
    mTh = np.ascontiguousarray(inputs["mem"].transpose(0, 2, 1))
    in_maps = []
    for c in range(ncores):
        in_maps.append({"xT": xTh[c * nseq:(c + 1) * nseq], "memT": mTh[c * nseq:(c + 1) * nseq],
                        "wp": wp, "cst": cst, "ctab": ctab, "lb": lb})
    res = run_bass_kernel_spmd(nc, in_maps, core_ids=list(range(ncores)))
    outT = np.concatenate([r["outT"] for r in res.results], axis=0)
    return np.ascontiguousarray(outT.transpose(0, 2, 1)).astype(np.float32)
```

```python
import numpy as np
from contextlib import ExitStack
import concourse.bass as bass
import concourse.mybir as mybir
from concourse.bass_utils import run_bass_kernel_spmd

F32 = mybir.dt.float32
BF16 = mybir.dt.bfloat16
ALU = mybir.AluOpType
AF = mybir.ActivationFunctionType

D = 1024
DFF = 2816
MEM = 256
EPS = 1e-6
NSLOT = 4
EPOCH = 4096
DEBUG = False
ENGINES = ("pe", "act", "dve", "pool", "sp")


class Op:
    __slots__ = ("eng", "fn", "reads", "writes", "dma", "idx", "waits", "signal", "count", "semkey")

    def __init__(self, eng, fn, reads, writes, dma):
        self.eng = eng
        self.fn = fn
        self.reads = tuple(reads)
        self.writes = tuple(writes)
        self.dma = dma
        self.waits = {}
        self.signal = False
        self.count = 0
        self.semkey = None


class Prog:
    def __init__(self):
        self.ops = []

    def add(self, eng, fn, reads=(), writes=(), dma=None):
        op = Op(eng, fn, reads, writes, dma)
        op.idx = len(self.ops)
        self.ops.append(op)
        return op

    def pe(self, fn, reads=(), writes=()):
        return self.add("pe", fn, reads, writes)

    def act(self, fn, reads=(), writes=()):
        return self.add("act", fn, reads, writes)

    def dve(self, fn, reads=(), writes=()):
        return self.add("dve", fn, reads, writes)

    def pool(self, fn, reads=(), writes=()):
        return self.add("pool", fn, reads, writes)

    def dma(self, q, stream, fn, reads=(), writes=()):
        return self.add(q, fn, reads, writes, dma=stream)

    def analyze(self, force_signal=()):
        last_writer = {}
        readers = {}
        need = [None] * len(self.ops)
        for op in self.ops:
            deps = set()
            for t in op.reads:
                w = last_writer.get(t)
                if w is not None:
                    deps.add(w)
            for t in op.writes:
                w = last_writer.get(t)
                if w is not None:
                    deps.add(w)
                deps.update(readers.get(t, ()))
            deps.discard(op.idx)
            best = {}
            nd = []
            for d in deps:
                a = self.ops[d]
                if a.dma is not None:
                    nd.append(d)
                    continue
                if op.dma is None and a.eng == op.eng and a.eng == "pe":
                    continue
                if d > best.get(a.eng, -1):
                    best[a.eng] = d
            nd.extend(best.values())
            for d in nd:
                self.ops[d].signal = True
            need[op.idx] = nd
            for t in op.writes:
                last_writer[t] = op.idx
                readers[t] = []
            for t in op.reads:
                if t not in op.writes:
                    readers.setdefault(t, []).append(op.idx)
        for op in force_signal:
            op.signal = True
        cnt = {}
        raw = {}
        for op in self.ops:
            if not op.signal:
                continue
            if op.dma is not None:
                key = ("dma", op.dma)
                cnt[key] = cnt.get(key, 0) + 16
                op.count = cnt[key]
            else:
                r = raw.get(op.eng, 0)
                key = ("eng", op.eng, r // EPOCH)
                raw[op.eng] = r + 1
                op.count = (r % EPOCH) + 1
                cnt[key] = op.count
            op.semkey = key
        self.semkeys = list(cnt.keys())
        self.final_counts = cnt
        waited = {e: {} for e in ENGINES}
        nw = 0
        for op in self.ops:
            w = {}
            for d in need[op.idx]:
                a = self.ops[d]
                if a.count > w.get(a.semkey, 0):
                    w[a.semkey] = a.count
            ww = {}
            for k, v in w.items():
                if waited[op.eng].get(k, 0) < v:
                    waited[op.eng][k] = v
                    ww[k] = v
                    nw += 1
            op.waits = ww
        self.n_waits = nw

    def emit(self, sems, block, final_waits=()):
        per = {e: [o for o in self.ops if o.eng == e] for e in ENGINES}

        def run(name):
            def body(eng):
                for op in per[name]:
                    for k, v in op.waits.items():
                        eng.wait_ge(sems[k], v)
                    ins = op.fn(eng)
                    if op.signal:
                        ins.then_inc(sems[op.semkey], 16 if op.dma is not None else 1)
                if name == "sp":
                    for k, v in final_waits:
                        eng.wait_ge(sems[k], v)
            return body

        block.tensor(run("pe"))
        block.scalar(run("act"))
        block.vector(run("dve"))
        block.gpsimd(run("pool"))
        block.sync(run("sp"))


FFN_GROUPS = ((0, 8), (8, 8), (16, 6))


def layer_pieces(l):
    out = []
    if l % 2 == 0:
        j = l // 2
        for m in range(8):
            blocks = [("conv_w_in", j, 0, 8, sec * 1024 + m * 128, 128, sec * 128) for sec in range(3)]
            out.append(("conv_in", m, blocks))
        for p in range(2):
            out.append(("out", ("conv", p, 8), [("conv_w_out", j, 0, 8, p * 512, 512, 0)]))
    else:
        j = l // 2
        for p in range(2):
            out.append(("og", p, [("hgrn_w_in", j, 0, 8, 3072 + p * 512, 512, 0)]))
        for hg in range(2):
            out.append(("gla_q", hg, [("hgrn_w_in", j, 0, 8, hg * 512, 512, 0)]))
            out.append(("gla_f", hg, [("hgrn_w_in", j, 0, 8, 1024 + hg * 512, 512, 0)]))
            out.append(("gla_i", hg, [("hgrn_w_in", j, 0, 8, 2048 + hg * 512, 512, 0)]))
        for p in range(2):
            out.append(("out", ("hgrn", p, 8), [("hgrn_w_out", j, 0, 8, p * 512, 512, 0)]))
    for p in range(4):
        out.append(("xa_kv", p, [("xattn_w_kv", l, 0, 8, p * 512, 512, 0)]))
    for p in range(2):
        out.append(("xa_q", p, [("xattn_w_q", l, 0, 8, p * 512, 512, 0)]))
    for p in range(2):
        out.append(("out", ("xa", p, 8), [("xattn_w_o", l, 0, 8, p * 512, 512, 0)]))
    for (f0, nf) in FFN_GROUPS:
        for q in range(nf // 2):
            fa, fb = f0 + 2 * q, f0 + 2 * q + 1
            blocks = [("ffn_w_in", l, 0, 8, fa * 128, 128, 0),
                      ("ffn_w_in", l, 0, 8, fb * 128, 128, 128),
                      ("ffn_w_in", l, 0, 8, DFF + fa * 128, 128, 256),
                      ("ffn_w_in", l, 0, 8, DFF + fb * 128, 128, 384)]
            out.append(("ffn_in", 2 * q, blocks))
        for p in range(2):
            out.append(("out", ("ffn", p, nf), [("ffn_w_out", l, f0 * 128, nf, p * 512, 512, 0)]))
    return out


def all_pieces():
    res = []
    for l in range(2):
        res.append(layer_pieces(l))
    return res


def pack_weights(inputs):
    pcs = all_pieces()
    n = sum(len(p) for p in pcs)
    wp = np.zeros((n, 128, 8, 512), np.float32)
    j = 0
    for lp in pcs:
        for (_, _, blocks) in lp:
            for (wn, li, r0, nkc, c0, ncol, dc) in blocks:
                w = inputs[wn][li]
                blk = w[r0:r0 + nkc * 128, c0:c0 + ncol].reshape(nkc, 128, ncol).transpose(1, 0, 2)
                wp[j, :, :nkc, dc:dc + ncol] = blk
            j += 1
    return wp.reshape(n, 128, 8 * 512)


def _fm(v):
    return np.asarray(v, np.float32).reshape(8, 128).T


CST_COLS = {}


def pack_consts(inputs):
    cols = []
    off = [0]

    def put(name, arr):
        CST_COLS[name] = (off[0], arr.shape[1])
        off[0] += arr.shape[1]
        cols.append(arr)

    for l in range(2):
        put("g_mix%d" % l, _fm(inputs["norm_mix"][l]))
        put("g_xa%d" % l, _fm(inputs["norm_xattn"][l]))
        put("g_mem%d" % l, _fm(inputs["norm_mem"][l]))
        put("g_ffn%d" % l, _fm(inputs["norm_ffn"][l]))
    put("g_fin", _fm(inputs["final_norm"]))
    put("gn", _fm(inputs["hgrn_norm"][0]))
    for jj in range(3):
        put("cw%d" % jj, _fm(inputs["conv_w"][0][jj]))
    return np.ascontiguousarray(np.concatenate(cols, axis=1))


def const_tables():
    s = np.arange(128)
    same = (s[:, None] // 64) == (s[None, :] // 64)
    tri = (same & (s[:, None] <= s[None, :])).astype(np.float32)
    ident = np.eye(128, dtype=np.float32)
    cm = np.zeros((128, 2), np.float32)
    cm[:64, 0] = 1.0
    cm[64:, 1] = 1.0
    return np.ascontiguousarray(np.concatenate([ident, tri, cm], axis=1))


def build_program(S, NSEQ, nsub=6, final=True):
    NT = S // 512
    NL = S // 128
    pcs = all_pieces()
    npieces = sum(len(p) for p in pcs)
    dummy = {"norm_mix": np.zeros((2, D)), "norm_xattn": np.zeros((2, D)), "norm_mem": np.zeros((2, D)),
             "norm_ffn": np.zeros((2, D)), "final_norm": np.zeros(D), "hgrn_norm": np.zeros((1, D)),
             "conv_w": np.zeros((1, 3, D))}
    ncst = pack_consts(dummy).shape[1]

    nc = bass.Bass("TRN2", target_bir_lowering=False)
    xd = nc.dram_tensor("xT", [NSEQ, D, S], F32, kind="ExternalInput").ap()
    md = nc.dram_tensor("memT", [NSEQ, D, MEM], F32, kind="ExternalInput").ap()
    wpd = nc.dram_tensor("wp", [npieces, 128, 8 * 512], F32, kind="ExternalInput").ap()
    cstd = nc.dram_tensor("cst", [128, ncst], F32, kind="ExternalInput").ap()
    ctabd = nc.dram_tensor("ctab", [128, 258], F32, kind="ExternalInput").ap()
    lbd = nc.dram_tensor("lb", [2, D], F32, kind="ExternalInput")
    od = nc.dram_tensor("outT", [NSEQ, D, S], F32, kind="ExternalOutput").ap()

    P = Prog()
    es = ExitStack()
    sb = lambda name, shape, dt: es.enter_context(nc.sbuf_tensor(name, shape, dt))
    xT = sb("xT_sb", [128, 8, S], F32)
    hT = sb("hT_sb", [128, 8, S], BF16)
    aT = sb("aT_sb", [128, 8, S], BF16)
    slots = [sb("slot%d" % i, [128, 8, 512], BF16) for i in range(NSLOT)]
    NF, NB = 11, 12
    Fp = [sb("F%d" % i, [128, 512], F32) for i in range(NF)]
    Bp = [sb("B%d" % i, [128, 512], BF16) for i in range(NB)]
    cst = sb("cst_sb", [128, ncst], F32)
    ctab = sb("ctab_sb", [128, 258], F32)
    identb = sb("identb", [128, 128], BF16)
    onesb = sb("onesb", [128, 128], BF16)
    maskb = sb("maskb", [128, 128], BF16)
    cmb = sb("cmb", [128, 2], BF16)
    oml = sb("oml", [128, D], F32)
    Sst = sb("Sst", [128, 512], F32)
    vcar = [sb("vcar%d" % i, [128, 514], F32) for i in range(2)]
    ebl = [sb("ebl%d" % i, [128, 8], F32) for i in range(2)]
    psb = [es.enter_context(nc.psum_tensor("ps%d" % i, [128, 512], F32)) for i in range(8)]
    ps_state = {"n": 0}
    DBG_T = {}
    if DEBUG:
        DBG_T["s1"] = sb("dbg_s1", [128, 512], F32)
        DBG_T["e"] = sb("dbg_e", [128, 8], F32)

    ps_state["ring"] = list(range(8))

    def nps():
        r = ps_state["ring"]
        b = r[ps_state["n"] % len(r)]
        ps_state["n"] += 1
        return b

    def cc(name, c=None):
        o, n = CST_COLS[name]
        if c is None:
            return cst[:, o:o + n]
        return cst[:, o + c:o + c + 1]

    tri32 = ctab[:, 128:256]
    cm32 = ctab[:, 256:258]

    P.dma("sp", "cst0", lambda e: e.dma_start(out=cst[:], in_=cstd), writes=["cst"])
    P.dma("sp", "cst1", lambda e: e.dma_start(out=ctab[:], in_=ctabd), writes=["ctab"])
    lb0 = bass.AP(lbd, 0, [[0, 128], [1, D]])
    P.dma("sp", "cst2", lambda e: e.dma_start(out=oml[:], in_=lb0), writes=["oml"])
    for hf in range(2):
        lb1 = bass.AP(lbd, D + hf * 512, [[0, 128], [1, 512]])
        P.dma("sp", "cst3_%d" % hf, lambda e, hf=hf, lb1=lb1: e.dma_start(out=Fp[4 + hf][:], in_=lb1), writes=[("F", 4 + hf)])
    P.dve(lambda e: e.tensor_copy(out=identb[:], in_=ctab[:, 0:128]), reads=["ctab"], writes=["identb"])
    P.dve(lambda e: e.tensor_copy(out=maskb[:], in_=ctab[:, 128:256]), reads=["ctab"], writes=["maskb"])
    P.dve(lambda e: e.memset(onesb[:], 1.0), writes=["onesb"])
    P.dve(lambda e: e.tensor_copy(out=cmb[:], in_=ctab[:, 256:258]), reads=["ctab"], writes=["cmb"])
    for hf in range(2):
        P.dve(lambda e, hf=hf: e.tensor_tensor(out=oml[:, hf * 512:(hf + 1) * 512], in0=oml[:, hf * 512:(hf + 1) * 512],
                                               in1=Fp[4 + hf][:], op=ALU.subtract),
              reads=["oml", ("F", 4 + hf)], writes=["oml"])
    P.act(lambda e: e.activation(out=oml[:], in_=oml[:], func=AF.Sigmoid), reads=["oml"], writes=["oml"])

    def xtok(c, tt):
        return ("x", c, tt)

    def htok(tt):
        return ("h", tt)

    def atok(c, tt):
        return ("a", c, tt)

    def norm_tile(src_fn, g_name, N, reads, out_fn, out_writes, ftmp, btmps):
        pb = nps()
        for c in range(8):
            bi = btmps[c % len(btmps)]
            P.act(lambda e, c=c, bi=bi: e.activation(out=Bp[bi][:, 0:N], in_=src_fn(c), func=AF.Square),
                  reads=[reads(c)], writes=[("B", bi)])
            P.pe(lambda e, c=c, bi=bi, pb=pb: e.matmul(psb[pb][:, 0:N], lhsT=onesb[:], rhs=Bp[bi][:, 0:N],
                                                     start=(c == 0), stop=(c == 7)),
                 reads=[("B", bi), "onesb"], writes=[("ps", pb)])
        P.act(lambda e, pb=pb: e.activation(out=Fp[ftmp][:, 0:N], in_=psb[pb][:, 0:N], func=AF.Ln,
                                            bias=EPS, scale=1.0 / D),
              reads=[("ps", pb)], writes=[("F", ftmp)])
        P.act(lambda e: e.activation(out=Fp[ftmp][:, 0:N], in_=Fp[ftmp][:, 0:N], func=AF.Exp, scale=-0.5),
              reads=[("F", ftmp)], writes=[("F", ftmp)])
        for c in range(8):
            P.dve(lambda e, c=c: e.scalar_tensor_tensor(out=out_fn(c), in0=src_fn(c), scalar=cc(g_name, c),
                                                        in1=Fp[ftmp][:, 0:N], op0=ALU.mult, op1=ALU.mult),
                  reads=[reads(c), ("F", ftmp), "cst"], writes=[out_writes(c)])

    def norm_phase(g_name):
        for tt in range(NT):
            ts = slice(tt * 512, (tt + 1) * 512)
            norm_tile(lambda c, ts=ts: xT[:, c, ts], g_name, 512, lambda c, tt=tt: xtok(c, tt),
                      lambda c, ts=ts: hT[:, c, ts], lambda c, tt=tt: htok(tt),
                      ftmp=tt % 2, btmps=[0, 1, 2, 3])

    def out_stage(slot, stok, info, after_tile=None):
        _, p, nkc = info
        for tt in range(NT):
            ts = slice(tt * 512, (tt + 1) * 512)
            for mo in range(4):
                pb = nps()
                ca = p * 4 + mo
                for kc in range(nkc):
                    P.pe(lambda e, kc=kc, mo=mo, pb=pb, ts=ts: e.matmul(
                        psb[pb][:], lhsT=slot[:, kc, mo * 128:(mo + 1) * 128], rhs=aT[:, kc, ts],
                        start=(kc == 0), stop=(kc == nkc - 1)),
                        reads=[stok, atok(kc, tt)], writes=[("ps", pb)])
                P.dve(lambda e, ca=ca, pb=pb, ts=ts: e.tensor_tensor(out=xT[:, ca, ts], in0=psb[pb][:],
                                                                     in1=xT[:, ca, ts], op=ALU.add),
                      reads=[("ps", pb), xtok(ca, tt)], writes=[xtok(ca, tt)])
            if after_tile is not None and tt >= 1:
                after_tile(tt - 1)
        if after_tile is not None:
            after_tile(NT - 1)

    def norm_one(g_name, tt):
        ts = slice(tt * 512, (tt + 1) * 512)
        norm_tile(lambda c, ts=ts: xT[:, c, ts], g_name, 512, lambda c, tt=tt: xtok(c, tt),
                  lambda c, ts=ts: hT[:, c, ts], lambda c, tt=tt: htok(tt),
                  ftmp=tt % 2, btmps=[0, 1, 2, 3])

    def conv_in(slot, stok, m):
        for tt in range(NT):
            ts = slice(tt * 512, (tt + 1) * 512)
            vb, vn = vcar[tt % 2], vcar[(tt + 1) % 2]
            vtk, vntk = ("vcar", tt % 2), ("vcar", (tt + 1) % 2)
            if tt == 0:
                P.dve(lambda e, vb=vb: e.memset(vb[:, 0:2], 0.0), writes=[vtk])
            pbs = []
            for sec in range(3):
                pb = nps()
                pbs.append(pb)
                for kc in range(8):
                    P.pe(lambda e, kc=kc, sec=sec, pb=pb, ts=ts: e.matmul(
                        psb[pb][:], lhsT=slot[:, kc, sec * 128:(sec + 1) * 128], rhs=hT[:, kc, ts],
                        start=(kc == 0), stop=(kc == 7)),
                        reads=[stok, htok(tt)], writes=[("ps", pb)])
            pgb, pgc, pu = pbs
            fu, fz = 2 + (tt % 2), 4 + (tt % 2)
            P.act(lambda e, pu=pu, fu=fu: e.activation(out=Fp[fu][:], in_=psb[pu][:], func=AF.Copy),
                  reads=[("ps", pu)], writes=[("F", fu)])
            P.dve(lambda e, pgc=pgc, fu=fu, vb=vb: e.tensor_tensor(out=vb[:, 2:514], in0=psb[pgc][:], in1=Fp[fu][:],
                                                                   op=ALU.mult),
                  reads=[("ps", pgc), ("F", fu)], writes=[vtk])
            if tt + 1 < NT:
                P.act(lambda e, vb=vb, vn=vn: e.activation(out=vn[:, 0:2], in_=vb[:, 512:514], func=AF.Copy),
                      reads=[vtk], writes=[vntk])
            P.act(lambda e, vb=vb, fz=fz, m=m: e.activation(out=Fp[fz][:], in_=vb[:, 2:514], func=AF.Copy,
                                                            scale=cc("cw2", m)),
                  reads=[vtk, "cst"], writes=[("F", fz)])
            P.dve(lambda e, vb=vb, fz=fz, m=m: e.scalar_tensor_tensor(out=Fp[fz][:], in0=vb[:, 1:513],
                                                                      scalar=cc("cw1", m), in1=Fp[fz][:],
                                                                      op0=ALU.mult, op1=ALU.add),
                  reads=[vtk, ("F", fz), "cst"], writes=[("F", fz)])
            P.dve(lambda e, vb=vb, fz=fz, m=m: e.scalar_tensor_tensor(out=Fp[fz][:], in0=vb[:, 0:512],
                                                                      scalar=cc("cw0", m), in1=Fp[fz][:],
                                                                      op0=ALU.mult, op1=ALU.add),
                  reads=[vtk, ("F", fz), "cst"], writes=[("F", fz)])
            P.dve(lambda e, pgb=pgb, fz=fz, m=m, ts=ts: e.tensor_tensor(out=aT[:, m, ts], in0=psb[pgb][:],
                                                                        in1=Fp[fz][:], op=ALU.mult),
                  reads=[("ps", pgb), ("F", fz)], writes=[atok(m, tt)])

    def ffn_in(slot, stok, lc0):
        for tt in range(NT):
            ts = slice(tt * 512, (tt + 1) * 512)
            for l2 in range(2):
                pa, pbk = nps(), nps()
                for (pb, co) in ((pa, l2 * 128), (pbk, 256 + l2 * 128)):
                    for kc in range(8):
                        P.pe(lambda e, kc=kc, pb=pb, co=co, ts=ts: e.matmul(
                            psb[pb][:], lhsT=slot[:, kc, co:co + 128], rhs=hT[:, kc, ts],
                            start=(kc == 0), stop=(kc == 7)),
                            reads=[stok, htok(tt)], writes=[("ps", pb)])
                fs = 6 + ((2 * tt + l2) % 2)
                lc = lc0 + l2
                P.act(lambda e, pa=pa, fs=fs: e.activation(out=Fp[fs][:], in_=psb[pa][:], func=AF.Silu),
                      reads=[("ps", pa)], writes=[("F", fs)])
                P.dve(lambda e, pbk=pbk, fs=fs, lc=lc, ts=ts: e.tensor_tensor(out=aT[:, lc, ts], in0=psb[pbk][:],
                                                                              in1=Fp[fs][:], op=ALU.mult),
                      reads=[("ps", pbk), ("F", fs)], writes=[atok(lc, tt)])

    def bview(i0, n, inner):
        raise NotImplementedError

    def memn_ap(c):
        return Bp[c // 2][:, (c % 2) * 256:(c % 2) * 256 + 256]

    def memn_tok(c):
        return ("B", c // 2)

    def kT_ap(dch):
        return Bp[4 + dch // 2][:, (dch % 2) * 256:(dch % 2) * 256 + 256]

    def kT_tok(dch):
        return ("B", 4 + dch // 2)

    def v_ap(mc, col0, n):
        half = col0 // 512
        return Bp[8 + mc * 2 + half][:, col0 % 512:col0 % 512 + n]

    def v_tok(mc, col0):
        return ("B", 8 + mc * 2 + col0 // 512)

    def xa_pre(seq, l):
        def mT(c):
            return Fp[c // 2][:, (c % 2) * 256:(c % 2) * 256 + 256]
        for c in range(8):
            P.dma("sp", "mem%d" % c, lambda e, c=c: e.dma_start(out=mT(c), in_=md[seq, c * 128:(c + 1) * 128, :]),
                  writes=[("F", c // 2)])
        norm_tile(mT, "g_mem%d" % l, MEM, lambda c: ("F", c // 2),
                  memn_ap, memn_tok, ftmp=8, btmps=[8, 9, 10, 11])

    def xa_kv(slot, stok, p):
        if p < 2:
            for j in range(4):
                dch = p * 4 + j
                pb = nps()
                for kc in range(8):
                    P.pe(lambda e, kc=kc, j=j, pb=pb: e.matmul(psb[pb][:, 0:MEM], lhsT=slot[:, kc, j * 128:(j + 1) * 128],
                                                             rhs=memn_ap(kc), start=(kc == 0), stop=(kc == 7)),
                         reads=[stok, memn_tok(kc)], writes=[("ps", pb)])
                P.act(lambda e, pb=pb, dch=dch: e.activation(out=kT_ap(dch), in_=psb[pb][:, 0:MEM], func=AF.Copy),
                      reads=[("ps", pb)], writes=[kT_tok(dch)])
        else:
            col0 = (p - 2) * 512
            for mc in range(2):
                pb = nps()
                for kc in range(8):
                    P.pe(lambda e, kc=kc, mc=mc, pb=pb: e.matmul(
                        psb[pb][:], lhsT=memn_ap(kc)[:, mc * 128:(mc + 1) * 128], rhs=slot[:, kc, :],
                        start=(kc == 0), stop=(kc == 7)),
                        reads=[stok, memn_tok(kc)], writes=[("ps", pb)])
                P.act(lambda e, pb=pb, mc=mc, col0=col0: e.activation(out=v_ap(mc, col0, 512), in_=psb[pb][:],
                                                                      func=AF.Copy),
                      reads=[("ps", pb)], writes=[v_tok(mc, col0)])

    def xa_q(slot, stok, p):
        for tt in range(NT):
            ts = slice(tt * 512, (tt + 1) * 512)
            for j in range(4):
                pb = nps()
                for kc in range(8):
                    P.pe(lambda e, kc=kc, j=j, pb=pb, ts=ts: e.matmul(
                        psb[pb][:], lhsT=slot[:, kc, j * 128:(j + 1) * 128], rhs=hT[:, kc, ts],
                        start=(kc == 0), stop=(kc == 7)),
                        reads=[stok, htok(tt)], writes=[("ps", pb)])
                P.act(lambda e, pb=pb, j=j: e.activation(out=Bp[j][:], in_=psb[pb][:], func=AF.Copy),
                      reads=[("ps", pb)], writes=[("B", j)])
            for hh in range(2):
                head = 2 * p + hh
                fpt = 4 + hh
                PT = Fp[fpt][:].bitcast(BF16)
                for mc in range(2):
                    pb = nps()
                    for jj in range(2):
                        dabs = head * 2 + jj
                        P.pe(lambda e, jj=jj, mc=mc, pb=pb, dabs=dabs, hh=hh: e.matmul(
                            psb[pb][:], lhsT=kT_ap(dabs)[:, mc * 128:(mc + 1) * 128], rhs=Bp[2 * hh + jj][:],
                            start=(jj == 0), stop=(jj == 1)),
                            reads=[kT_tok(dabs), ("B", 2 * hh + jj)], writes=[("ps", pb)])
                    P.act(lambda e, pb=pb, mc=mc, PT=PT: e.activation(out=PT[:, mc * 512:(mc + 1) * 512], in_=psb[pb][:],
                                                                      func=AF.Exp, scale=1.0 / 16.0),
                          reads=[("ps", pb)], writes=[("F", fpt)])
                pd = nps()
                for mc in range(2):
                    P.pe(lambda e, mc=mc, pd=pd, PT=PT: e.matmul(psb[pd][:], lhsT=onesb[:],
                                                                 rhs=PT[:, mc * 512:(mc + 1) * 512],
                                                                 start=(mc == 0), stop=(mc == 1)),
                         reads=[("F", fpt), "onesb"], writes=[("ps", pd)])
                frd = 6 + hh
                P.act(lambda e, pd=pd, frd=frd: e.activation(out=Fp[frd][:], in_=psb[pd][:], func=AF.Ln),
                      reads=[("ps", pd)], writes=[("F", frd)])
                P.act(lambda e, frd=frd: e.activation(out=Fp[frd][:], in_=Fp[frd][:], func=AF.Exp, scale=-1.0),
                      reads=[("F", frd)], writes=[("F", frd)])
                for jj in range(2):
                    dabs = head * 2 + jj
                    po = nps()
                    for mc in range(2):
                        P.pe(lambda e, mc=mc, po=po, PT=PT, head=head, jj=jj: e.matmul(
                            psb[po][:], lhsT=v_ap(mc, head * 256 + jj * 128, 128),
                            rhs=PT[:, mc * 512:(mc + 1) * 512], start=(mc == 0), stop=(mc == 1)),
                            reads=[("F", fpt), v_tok(mc, head * 256 + jj * 128)], writes=[("ps", po)])
                    P.dve(lambda e, po=po, frd=frd, dabs=dabs, ts=ts: e.tensor_tensor(
                        out=aT[:, dabs, ts], in0=psb[po][:], in1=Fp[frd][:], op=ALU.mult),
                        reads=[("ps", po), ("F", frd)], writes=[atok(dabs, tt)])

    def og_stage(slot, stok, p):
        for tt in range(NT):
            ts = slice(tt * 512, (tt + 1) * 512)
            for j in range(4):
                pb = nps()
                ca = p * 4 + j
                for kc in range(8):
                    P.pe(lambda e, kc=kc, j=j, pb=pb, ts=ts: e.matmul(
                        psb[pb][:], lhsT=slot[:, kc, j * 128:(j + 1) * 128], rhs=hT[:, kc, ts],
                        start=(kc == 0), stop=(kc == 7)),
                        reads=[stok, htok(tt)], writes=[("ps", pb)])
                P.act(lambda e, pb=pb, ca=ca, ts=ts: e.activation(out=aT[:, ca, ts], in_=psb[pb][:], func=AF.Silu),
                      reads=[("ps", pb)], writes=[atok(ca, tt)])

    def gla(sq_, sf_, si_, tq, tf, ti, hg):
        hc = slice(hg * 512, (hg + 1) * 512)
        FQ, FK, FL, FH = (0, 4), (1, 5), (2, 6), (7, 10)
        P.dve(lambda e: e.memset(Sst[:], 0.0), writes=["S"])
        P.dve(lambda e: e.memset(Bp[11][:], 0.0), writes=[("B", 11)])
        ps_state["ring"] = list(range(6))
        st = {}

        def front_A(tl):
            par = tl % 2
            tk = slice(tl * 128, (tl + 1) * 128)
            tt = tl // 4
            pq, pf, pi_ = nps(), nps(), nps()
            for (pb, sl, stk) in ((pq, sq_, tq), (pf, sf_, tf), (pi_, si_, ti)):
                for kc in range(8):
                    P.pe(lambda e, kc=kc, pb=pb, sl=sl: e.matmul(psb[pb][:], lhsT=hT[:, kc, tk], rhs=sl[:, kc, :],
                                                               start=(kc == 0), stop=(kc == 7)),
                         reads=[stk, htok(tt)], writes=[("ps", pb)])
            fq, fk, fl, bv = FQ[par], FK[par], FL[par], 3 + par
            P.act(lambda e: e.activation(out=Fp[fk][:], in_=psb[pf][:], func=AF.Sigmoid, scale=-1.0),
                  reads=[("ps", pf)], writes=[("F", fk)])
            P.act(lambda e: e.activation(out=Fp[fq][:], in_=psb[pq][:], func=AF.Sigmoid),
                  reads=[("ps", pq)], writes=[("F", fq)])
            P.pool(lambda e: e.tensor_tensor(out=Fp[fk][:], in0=Fp[fk][:], in1=oml[:, hc], op=ALU.mult),
                   reads=[("F", fk), "oml"], writes=[("F", fk)])
            P.dve(lambda e: e.tensor_tensor(out=Fp[fq][:], in0=psb[pq][:], in1=Fp[fq][:], op=ALU.mult),
                  reads=[("ps", pq), ("F", fq)], writes=[("F", fq)])
            P.act(lambda e: e.activation(out=Bp[bv][:], in_=psb[pi_][:], func=AF.Copy),
                  reads=[("ps", pi_)], writes=[("B", bv)])
            P.act(lambda e: e.activation(out=Fp[fl][:], in_=Fp[fk][:], func=AF.Ln, scale=-1.0, bias=1.0),
                  reads=[("F", fk)], writes=[("F", fl)])
            fh = FH[par]
            HL = Fp[fh][:].bitcast(BF16)
            P.dve(lambda e: e.tensor_copy(out=HL[:, 0:512], in_=Fp[fl][:]), reads=[("F", fl)], writes=[("F", fh)])
            P.pool(lambda e: e.tensor_tensor(out=HL[:, 512:1024], in0=Fp[fl][:], in1=HL[:, 0:512], op=ALU.subtract),
                   reads=[("F", fl), ("F", fh)], writes=[("F", fh)])

        def front_BC(tl):
            par = tl % 2
            fq, fk, fl = FQ[par], FK[par], FL[par]
            bk, bq, bkT, bqT = 0 + par, 2, 5 + par, 7 + par
            fh = FH[par]
            HL = Fp[fh][:].bitcast(BF16)
            pbb = nps()
            for hl in range(2):
                P.pe(lambda e, hl=hl: e.matmul(psb[pbb][:], lhsT=maskb[:], rhs=HL[:, hl * 512:(hl + 1) * 512],
                                               start=(hl == 0), stop=(hl == 1)),
                     reads=[("F", fh), "maskb"], writes=[("ps", pbb)])
            pbl = nps()
            for hd in range(4):
                for hl in range(2):
                    P.pe(lambda e, hd=hd, hl=hl: e.matmul(psb[pbl][:, 2 * hd:2 * hd + 2],
                                                         lhsT=HL[:, hl * 512 + hd * 128:hl * 512 + (hd + 1) * 128],
                                                         rhs=cmb[:], start=(hd == 0 and hl == 0),
                                                         stop=(hd == 3 and hl == 1)),
                         reads=[("F", fh), "cmb"], writes=[("ps", pbl)])
            P.act(lambda e: e.activation(out=Fp[3][:], in_=psb[pbb][:], func=AF.Exp, scale=-1.0),
                  reads=[("ps", pbb)], writes=[("F", 3)])
            P.dve(lambda e: e.tensor_tensor(out=Bp[bk][:], in0=Fp[fk][:], in1=Fp[3][:], op=ALU.mult),
                  reads=[("F", fk), ("F", 3)], writes=[("B", bk)])
            P.act(lambda e: e.activation(out=Fp[3][:], in_=psb[pbb][:], func=AF.Exp),
                  reads=[("ps", pbb)], writes=[("F", 3)])
            P.dve(lambda e: e.tensor_tensor(out=Bp[bq][:], in0=Fp[fq][:], in1=Fp[3][:], op=ALU.mult),
                  reads=[("F", fq), ("F", 3)], writes=[("B", bq)])
            P.act(lambda e: e.activation(out=ebl[par][:], in_=psb[pbl][:, 0:8], func=AF.Exp),
                  reads=[("ps", pbl)], writes=[("ebl", par)])
            for (src, dst) in ((bk, bkT), (bq, bqT)):
                pT = nps()
                pv = psb[pT][:].bitcast(BF16)
                for hd in range(4):
                    P.pe(lambda e, hd=hd, src=src, pv=pv: e.transpose(pv[:, hd * 128:(hd + 1) * 128],
                                                                      Bp[src][:, hd * 128:(hd + 1) * 128], identb[:]),
                         reads=[("B", src), "identb"], writes=[("ps", pT)])
                if dst == bkT:
                    P.act(lambda e, pv=pv, dst=dst: e.activation(out=Bp[dst][:], in_=pv[:, 0:512], func=AF.Copy),
                          reads=[("ps", pT)], writes=[("B", dst)])
                else:
                    P.dve(lambda e, pv=pv, dst=dst: e.tensor_copy(out=Bp[dst][:], in_=pv[:, 0:512]),
                          reads=[("ps", pT)], writes=[("B", dst)])

        def s_update(par, c, pst):
            P.dve(lambda e: e.tensor_tensor(out=Sst[:], in0=psb[pst][:], in1=Sst[:], op=ALU.add),
                  reads=[("ps", pst), "S"], writes=["S"])
            eb4 = bass.AP(ebl[par], c, [[ebl[par].ap().ap[0][0], 128], [2, 4], [0, 128]])
            P.dve(lambda e: e.tensor_tensor(out=Sst[:].rearrange("p (h t) -> p h t", h=4),
                                            in0=Sst[:].rearrange("p (h t) -> p h t", h=4),
                                            in1=eb4, op=ALU.mult),
                  reads=["S", ("ebl", par)], writes=["S"])
            P.pool(lambda e: e.tensor_copy(out=Bp[11][:], in_=Sst[:]),
                   reads=["S"], writes=[("B", 11)])

        def inter(par, c, po):
            bqT = 7 + par
            for hd in range(4):
                h_ = slice(hd * 128, (hd + 1) * 128)
                oc = slice(hd * 128 + c * 64, hd * 128 + (c + 1) * 64)
                P.pe(lambda e, hd=hd, h_=h_, oc=oc: e.matmul(psb[po][:, oc], lhsT=Bp[11][:, h_], rhs=Bp[bqT][:, oc],
                                                             start=False, stop=(c == 1 and hd == 3)),
                     reads=[("B", 11), ("B", bqT)], writes=[("ps", po)])

        def back_1(tl):
            par = tl % 2
            bk, bv, bkT, bqT = 0 + par, 3 + par, 5 + par, 7 + par
            pat = nps()
            for hd in range(4):
                h_ = slice(hd * 128, (hd + 1) * 128)
                P.pe(lambda e, hd=hd, h_=h_: e.matmul(psb[pat][:, h_], lhsT=Bp[bkT][:, h_], rhs=Bp[bqT][:, h_],
                                                      start=(hd == 0), stop=(hd == 3)),
                     reads=[("B", bkT), ("B", bqT)], writes=[("ps", pat)])
            mask4 = maskb[:, :].unsqueeze(1).to_broadcast([128, 4, 128])
            P.dve(lambda e: e.tensor_tensor(out=Bp[9][:].rearrange("p (h t) -> p h t", h=4),
                                            in0=psb[pat][:].rearrange("p (h t) -> p h t", h=4),
                                            in1=mask4, op=ALU.mult),
                  reads=[("ps", pat), "maskb"], writes=[("B", 9)])
            psts = []
            for c in range(2):
                cs = slice(c * 64, (c + 1) * 64)
                pst = nps()
                psts.append(pst)
                for hd in range(4):
                    h_ = slice(hd * 128, (hd + 1) * 128)
                    P.pe(lambda e, hd=hd, h_=h_, cs=cs, pst=pst: e.matmul(psb[pst][:, h_], lhsT=Bp[bk][cs, h_],
                                                                          rhs=Bp[bv][cs, h_],
                                                                          start=(hd == 0), stop=(hd == 3)),
                         reads=[("B", bk), ("B", bv)], writes=[("ps", pst)])
            po = 6 + par
            for hd in range(4):
                h_ = slice(hd * 128, (hd + 1) * 128)
                P.pe(lambda e, hd=hd, h_=h_: e.matmul(psb[po][:, h_], lhsT=Bp[bv][:, h_], rhs=Bp[9][:, h_],
                                                      start=(hd == 0), stop=False),
                     reads=[("B", bv), ("B", 9)], writes=[("ps", po)])
            inter(par, 0, po)
            s_update(par, 0, psts[0])
            st[tl] = (po, psts[1])

        def back_2(tl):
            par = tl % 2
            po, pst1 = st[tl]
            inter(par, 1, po)
            s_update(par, 1, pst1)
            P.act(lambda e: e.activation(out=Bp[10][:], in_=psb[po][:], func=AF.Square),
                  reads=[("ps", po)], writes=[("B", 10)])

        def back_3(tl):
            tk = slice(tl * 128, (tl + 1) * 128)
            tt = tl // 4
            po, _ = st[tl]
            pss = nps()
            P.pe(lambda e: e.matmul(psb[pss][:], lhsT=onesb[:], rhs=Bp[10][:], start=True, stop=True),
                 reads=[("B", 10), "onesb"], writes=[("ps", pss)])
            P.act(lambda e: e.activation(out=Fp[9][:], in_=psb[pss][:], func=AF.Ln, bias=EPS, scale=1.0 / 128.0),
                  reads=[("ps", pss)], writes=[("F", 9)])
            P.act(lambda e: e.activation(out=Fp[9][:], in_=Fp[9][:], func=AF.Exp, scale=-0.5),
                  reads=[("F", 9)], writes=[("F", 9)])
            for hd in range(4):
                h_ = slice(hd * 128, (hd + 1) * 128)
                P.dve(lambda e, hd=hd, h_=h_: e.scalar_tensor_tensor(out=Fp[8][:, h_], in0=psb[po][:, h_],
                                                                     scalar=cc("gn", hg * 4 + hd), in1=Fp[9][:, h_],
                                                                     op0=ALU.mult, op1=ALU.mult),
                      reads=[("ps", po), ("F", 9), "cst"], writes=[("F", 8)])
            P.pool(lambda e: e.tensor_tensor(out=aT[:, hg * 4:(hg + 1) * 4, tk],
                                             in0=Fp[8][:].rearrange("p (h t) -> p h t", h=4),
                                             in1=aT[:, hg * 4:(hg + 1) * 4, tk], op=ALU.mult),
                   reads=[("F", 8)] + [atok(hg * 4 + hd, tt) for hd in range(4)],
                   writes=[atok(hg * 4 + hd, tt) for hd in range(4)])

        front_A(0)
        front_BC(0)
        if NL > 1:
            front_A(1)
        for tl in range(NL):
            back_1(tl)
            if tl + 2 < NL:
                front_A(tl + 2)
            back_2(tl)
            if tl + 1 < NL:
                front_BC(tl + 1)
            back_3(tl)
        ps_state["ring"] = list(range(8))

    steps = []
    outs = []
    pidx0 = [0, len(pcs[0])]
    SUBOF = {"conv_in": 0, "og": 0, "gla_q": 0, "gla_f": 0, "gla_i": 0, "xa_kv": 1, "xa_q": 1, "ffn_in": 2}
    for seq in range(NSEQ):
        def load_x(seq=seq):
            for c in range(8):
                P.dma("sp", "xin%d" % c, lambda e, c=c: e.dma_start(out=xT[:, c, :], in_=xd[seq, c * 128:(c + 1) * 128, :]),
                      writes=[xtok(c, tt) for tt in range(NT)])
        steps.append((None, lambda s, t, f=load_x: f()))

        def fin_tile(tt, seq=seq):
            ts = slice(tt * 512, (tt + 1) * 512)
            if final:
                norm_tile(lambda c, ts=ts: xT[:, c, ts], "g_fin", 512, lambda c, tt=tt: xtok(c, tt),
                          lambda c, ts=ts: xT[:, c, ts], lambda c, tt=tt: xtok(c, tt),
                          ftmp=tt % 2, btmps=[0, 1, 2, 3])
            for c in range(8):
                outs.append(P.dma("sp", "out%d" % c, lambda e, c=c, ts=ts: e.dma_start(
                    out=od[seq, c * 128:(c + 1) * 128, ts], in_=xT[:, c, ts]),
                    reads=[xtok(c, tt)], writes=[("out", seq, c, tt)]))

        subs = []
        for l in range(2):
            for j, (kind, info, _) in enumerate(pcs[l]):
                sub = SUBOF.get(kind)
                if kind == "out":
                    sub = {"conv": 0, "hgrn": 0, "xa": 1, "ffn": 2}[info[0]]
                gsub = l * 3 + sub
                if gsub >= nsub:
                    continue
                if not subs or subs[-1][0] != gsub:
                    subs.append((gsub, l, sub, []))
                subs[-1][3].append((pidx0[l] + j, kind, info))

        def gname_of(k):
            _, l, sub, _ = subs[k]
            return ["g_mix%d", "g_xa%d", "g_ffn%d"][sub] % l

        if not subs:
            steps.append((None, lambda s, t, f=fin_tile: [f(tt) for tt in range(NT)]))
        for k, (gsub, l, sub, plist) in enumerate(subs):
            if k == 0:
                steps.append((None, lambda s, t, g=gname_of(0): norm_phase(g)))
            if sub == 1:
                steps.append((None, lambda s, t, seq=seq, l=l: xa_pre(seq, l)))
            gla_hold = {}
            for ip, (pi, kind, info) in enumerate(plist):
                last = (ip == len(plist) - 1)
                if kind == "conv_in":
                    steps.append((pi, lambda s, t, m=info: conv_in(s, t, m)))
                elif kind == "out":
                    if last:
                        if k + 1 < len(subs):
                            hook = (lambda tt, g=gname_of(k + 1): norm_one(g, tt))
                        else:
                            hook = fin_tile
                        steps.append((pi, lambda s, t, info=info, hook=hook: out_stage(s, t, info, hook)))
                    else:
                        steps.append((pi, lambda s, t, info=info: out_stage(s, t, info)))
                elif kind == "ffn_in":
                    steps.append((pi, lambda s, t, lc0=info: ffn_in(s, t, lc0)))
                elif kind == "xa_kv":
                    steps.append((pi, lambda s, t, p=info: xa_kv(s, t, p)))
                elif kind == "xa_q":
                    steps.append((pi, lambda s, t, p=info: xa_q(s, t, p)))
                elif kind == "og":
                    steps.append((pi, lambda s, t, p=info: og_stage(s, t, p)))
                elif kind == "gla_q":
                    steps.append((pi, lambda s, t, gh=gla_hold: gh.update(q=(s, t))))
                elif kind == "gla_f":
                    steps.append((pi, lambda s, t, gh=gla_hold: gh.update(f=(s, t)), 1))
                elif kind == "gla_i":
                    steps.append((pi, lambda s, t, gh=gla_hold, hg=info: gla(gh["q"][0], gh["f"][0], s,
                                                                           gh["q"][1], gh["f"][1], t, hg), 2))

    piece_steps = [i for i, st in enumerate(steps) if st[0] is not None]
    nissued = [0]

    def issue_piece(k):
        si = piece_steps[k]
        pi = steps[si][0]
        slot = slots[k % NSLOT]
        stok = ("slot", k % NSLOT)
        P.dma("pool", "w%d" % (k % NSLOT), lambda e, pi=pi, slot=slot: e.dma_start(
            out=slot[:].rearrange("p a b -> p (a b)"), in_=wpd[pi]), writes=[stok])

    kcur = 0
    for i, st in enumerate(steps):
        pi, fn = st[0], st[1]
        hold = st[2] if len(st) > 2 else 0
        if pi is not None:
            while nissued[0] < len(piece_steps) and nissued[0] < kcur - hold + NSLOT:
                issue_piece(nissued[0])
                nissued[0] += 1
            fn(slots[kcur % NSLOT], ("slot", kcur % NSLOT))
            kcur += 1
        else:
            fn(None, None)

    P.analyze(force_signal=outs)
    sems = {k: es.enter_context(nc.semaphore("s_" + "_".join(str(x) for x in k))) for k in P.semkeys}
    fw = [(("dma", "out%d" % c), P.final_counts[("dma", "out%d" % c)]) for c in range(8)]
    with nc.Block() as block:
        P.emit(sems, block, final_waits=fw)
    es.close()
    return nc, P


_CACHE = {}


def kernel(**inputs):
    inputs = {k: np.asarray(v) for k, v in inputs.items()}
    x = inputs["x"]
    B, S, _ = x.shape
    ncores = 8
    nseq = B // ncores
    if "nc" not in _CACHE:
        _CACHE["nc"] = build_program(S, nseq)[0]
    nc = _CACHE["nc"]
    wp = pack_weights(inputs)
    cst = pack_consts(inputs)
    ctab = const_tables()
    lb = np.ascontiguousarray(inputs["hgrn_lb"], dtype=np.float32)
    xTh = np.ascontiguousarray(x.transpose(0, 2, 1))
    mTh = np.ascontiguousarray(inputs["mem"].transpose(0, 2, 1))
    in_maps = []
    for c in range(ncores):
        in_maps.append({"xT": xTh[c * nseq:(c + 1) * nseq], "memT": mTh[c * nseq:(c + 1) * nseq],
                        "wp": wp, "cst": cst, "ctab": ctab, "lb": lb})
    res = run_bass_kernel_spmd(nc, in_maps, core_ids=list(range(ncores)))
    outT = np.concatenate([r["outT"] for r in res.results], axis=0)
    return np.ascontiguousarray(outT.transpose(0, 2, 1)).astype(np.float32)
```
